# Optimizing a Trainium2 kernel written in Bass

```python
import math
import jax
import jax.numpy as jnp
from jax import lax
import numpy as np

D_MODEL = 1024
BATCH = 2
SEQ = 8192
DEPTH = 2
DEC_BATCH = 128
DEC_SEQ = 4
PAST_LEN = 8192
PAGE_SIZE = 128

N_EVEN = (DEPTH + 1) // 2
N_ODD = DEPTH // 2
NORM_EPS = 1e-6

S5_WIDTH = D_MODEL // 2
S5_GROUP = 16
S5_GROUPS = S5_WIDTH // S5_GROUP
S5_STATE = 64
S5_DT_MIN = 1e-3
S5_DT_MAX = 1e-1

GLA_HEADS = 4
GLA_V_WIDTH = D_MODEL // 2
GLA_K_WIDTH = GLA_V_WIDTH // 2
GLA_HEAD_K = GLA_K_WIDTH // GLA_HEADS
GLA_HEAD_V = GLA_V_WIDTH // GLA_HEADS
GLA_GATE_RANK = 16
GLA_GATE_TAU = 16.0
GLA_CHUNK = 64

EVEN_COLS = (S5_WIDTH, S5_WIDTH, GLA_K_WIDTH, GLA_K_WIDTH, GLA_V_WIDTH, GLA_GATE_RANK, GLA_V_WIDTH)
EVEN_IN = sum(EVEN_COLS)
EVEN_SPLIT = tuple(np.cumsum(EVEN_COLS[:-1]).tolist())
EVEN_MIX = S5_WIDTH + GLA_V_WIDTH

SWA_HEADS = 16
SWA_KV_HEADS = 2
SWA_GROUP = SWA_HEADS // SWA_KV_HEADS
SWA_HEAD_DIM = 64
SWA_WINDOW = 128
ROPE_THETA = 10000.0
SWA_Q_WIDTH = SWA_HEADS * SWA_HEAD_DIM
SWA_KV_WIDTH = SWA_KV_HEADS * SWA_HEAD_DIM
ODD_COLS = (SWA_Q_WIDTH, SWA_KV_WIDTH, SWA_KV_WIDTH, SWA_Q_WIDTH)
ODD_IN = sum(ODD_COLS)
ODD_SPLIT = tuple(np.cumsum(ODD_COLS[:-1]).tolist())

kernel_name = 'hybrid_s5_gla_swa_sink_step'


def rms_norm(x, g):
    xf = x.astype(jnp.float32)
    y = xf * lax.rsqrt(jnp.mean(xf * xf, axis=-1, keepdims=True) + NORM_EPS)
    return (y * g.astype(jnp.float32)).astype(x.dtype)


def s5_discretize(lam_re, lam_im, b_re, b_im, log_dt):
    lam = lax.complex(lam_re.astype(jnp.float32), lam_im.astype(jnp.float32))
    dt = jnp.exp(log_dt.astype(jnp.float32))[:, None]
    lam_bar = jnp.exp(lam * dt)
    b = lax.complex(b_re.astype(jnp.float32), b_im.astype(jnp.float32))
    b_bar = ((lam_bar - 1.0) / lam)[..., None] * b
    return lam_bar, b_bar


def _linear_combine(e1, e2):
    a1, b1 = e1
    a2, b2 = e2
    return a2 * a1, a2 * b1 + b2


def s5_scan(u, state0, lam_bar, b_bar, c_re, c_im, d):
    n, l, _ = u.shape
    ug = u.reshape(n, l, S5_GROUPS, S5_GROUP).astype(jnp.complex64)
    bu = jnp.einsum('gpc,nlgc->nlgp', b_bar, ug)
    bu = bu.at[:, 0].add(lam_bar * state0)
    a = jnp.broadcast_to(lam_bar, bu.shape)
    _, xs = lax.associative_scan(_linear_combine, (a, bu), axis=1)
    c = lax.complex(c_re.astype(jnp.float32), c_im.astype(jnp.float32))
    y = jnp.einsum('gcp,nlgp->nlgc', c, xs).real.reshape(n, l, S5_WIDTH) + d.astype(jnp.float32) * u
    return y, xs[:, -1]


def gla_chunked(q, k, v, log_a, s0):
    n, l = q.shape[:2]
    c = math.gcd(l, GLA_CHUNK)
    nc = l // c

    def to_chunks(t):
        return t.reshape(n, nc, c, GLA_HEADS, t.shape[-1]).transpose(1, 0, 3, 2, 4)

    qc, kc, vc, ac = to_chunks(q), to_chunks(k), to_chunks(v), to_chunks(log_a)
    causal = jnp.tril(jnp.ones((c, c), dtype=bool))

    def step(s, inp):
        qi, ki, vi, ai = inp
        b = jnp.cumsum(ai, axis=-2)
        b_last = b[..., -1:, :]
        q_dec = qi * jnp.exp(b)
        k_dec = ki * jnp.exp(-b)
        att = jnp.where(causal, jnp.einsum('nhik,nhjk->nhij', q_dec, k_dec), 0.0)
        o = jnp.einsum('nhij,nhjv->nhiv', att, vi) + jnp.einsum('nhik,nhkv->nhiv', q_dec, s)
        k_tail = ki * jnp.exp(b_last - b)
        s_new = jnp.exp(b_last[..., 0, :])[..., None] * s + jnp.einsum('nhjk,nhjv->nhkv', k_tail, vi)
        return s_new, o

    s_final, oc = lax.scan(step, s0, (qc, kc, vc, ac))
    o = oc.transpose(1, 0, 3, 2, 4).reshape(n, l, GLA_HEADS, GLA_HEAD_V)
    return o, s_final


def even_layer(x, s5_state0, gla_state0, p):
    n, l, _ = x.shape
    proj = (rms_norm(x, p['norm_g']) @ p['w_in']).astype(jnp.float32)
    u, gate_a, q, k, v, a_low, gate_b = jnp.split(proj, EVEN_SPLIT, axis=-1)
    lam_bar, b_bar = s5_discretize(p['lam_re'], p['lam_im'], p['b_re'], p['b_im'], p['log_dt'])
    y, s5_last = s5_scan(u, s5_state0, lam_bar, b_bar, p['c_re'], p['c_im'], p['d'])
    z = jax.nn.gelu(y)
    z = z * jax.nn.sigmoid(z @ p['w_glu'].astype(jnp.float32) + p['b_glu'].astype(jnp.float32))
    out_a = z * jax.nn.silu(gate_a)
    q = q.reshape(n, l, GLA_HEADS, GLA_HEAD_K) * (GLA_HEAD_K ** -0.5)
    k = k.reshape(n, l, GLA_HEADS, GLA_HEAD_K)
    v = v.reshape(n, l, GLA_HEADS, GLA_HEAD_V)
    gate_logit = a_low @ p['w_gate'].astype(jnp.float32) + p['b_gate'].astype(jnp.float32)
    log_a = (jax.nn.log_sigmoid(gate_logit) / GLA_GATE_TAU).reshape(n, l, GLA_HEADS, GLA_HEAD_K)
    o, gla_last = gla_chunked(q, k, v, log_a, gla_state0)
    out_b = rms_norm(o, p['gla_norm_g']).reshape(n, l, GLA_V_WIDTH) * jax.nn.silu(gate_b)
    mix = jnp.concatenate([out_a, out_b], axis=-1).astype(x.dtype) @ p['w_out']
    return x + mix, s5_last, gla_last


def rope(x, pos):
    half = SWA_HEAD_DIM // 2
    inv_freq = ROPE_THETA ** (-jnp.arange(half, dtype=jnp.float32) / half)
    ang = pos.astype(jnp.float32)[:, None] * inv_freq[None, :]
    cos = jnp.cos(ang)[:, None, :]
    sin = jnp.sin(ang)[:, None, :]
    x = x.astype(jnp.float32)
    x1, x2 = x[..., :half], x[..., half:]
    return jnp.concatenate([x1 * cos - x2 * sin, x2 * cos + x1 * sin], axis=-1)


def sink_attention(q, k, v, q_pos, k_pos, sinks):
    s = jnp.einsum('...qhgd,...khd->...hgqk', q, k) * (SWA_HEAD_DIM ** -0.5)
    diff = q_pos[..., :, None] - k_pos[..., None, :]
    allowed = (diff >= 0) & (diff < SWA_WINDOW) & (k_pos[..., None, :] >= 0)
    s = jnp.where(allowed[..., None, None, :, :], s, -jnp.inf)
    sink = sinks.astype(jnp.float32).reshape(SWA_KV_HEADS, SWA_GROUP, 1, 1)
    m = jnp.maximum(jnp.max(s, axis=-1, keepdims=True), sink)
    pr = jnp.exp(s - m)
    denom = jnp.sum(pr, axis=-1, keepdims=True) + jnp.exp(sink - m)
    return jnp.einsum('...hgqk,...khd->...qhgd', pr / denom, v)


def swa_project(x, pos, p):
    n, l, _ = x.shape
    proj = (rms_norm(x, p['norm_g']) @ p['w_in']).astype(jnp.float32)
    q, k, v, gate = jnp.split(proj, ODD_SPLIT, axis=-1)
    q = rope(rms_norm(q.reshape(n, l, SWA_HEADS, SWA_HEAD_DIM), p['q_norm_g']), pos)
    k = rope(rms_norm(k.reshape(n, l, SWA_KV_HEADS, SWA_HEAD_DIM), p['k_norm_g']), pos)
    v = v.reshape(n, l, SWA_KV_HEADS, SWA_HEAD_DIM)
    return q, k, v, gate


def swa_output(x, o, gate, p):
    n, l, _ = x.shape
    o = o.reshape(n, l, SWA_Q_WIDTH) * jax.nn.silu(gate)
    return x + o.astype(x.dtype) @ p['w_out']


def odd_layer_prompt(x, p):
    n, l, _ = x.shape
    pos = jnp.arange(l, dtype=jnp.int32)
    q, k, v, gate = swa_project(x, pos, p)
    nb = l // SWA_WINDOW
    qb = q.reshape(n, nb, SWA_WINDOW, SWA_KV_HEADS, SWA_GROUP, SWA_HEAD_DIM)

    def band(t):
        tb = t.reshape(n, nb, SWA_WINDOW, SWA_KV_HEADS, SWA_HEAD_DIM)
        prev = jnp.pad(tb, ((0, 0), (1, 0), (0, 0), (0, 0), (0, 0)))[:, :-1]
        return jnp.concatenate([prev, tb], axis=2)

    pos_b = pos.reshape(nb, SWA_WINDOW)
    k_pos = jnp.concatenate([pos_b - SWA_WINDOW, pos_b], axis=1)
    o = sink_attention(qb, band(k), band(v), pos_b, k_pos, p['sinks'])
    cache_len = min(SWA_WINDOW, l)
    return swa_output(x, o, gate, p), k[:, -cache_len:], v[:, -cache_len:]


def odd_layer_sample(x, k_cache, v_cache, p):
    n, l, _ = x.shape
    c = k_cache.shape[1]
    pos = PAST_LEN + jnp.arange(l, dtype=jnp.int32)
    q, k, v, gate = swa_project(x, pos, p)
    kk = jnp.concatenate([k_cache.astype(jnp.float32), k], axis=1)
    vv = jnp.concatenate([v_cache.astype(jnp.float32), v], axis=1)
    k_pos = (PAST_LEN - c) + jnp.arange(c + l, dtype=jnp.int32)
    qg = q.reshape(n, l, SWA_KV_HEADS, SWA_GROUP, SWA_HEAD_DIM)
    o = sink_attention(qg, kk, vv, pos, k_pos, p['sinks'])
    return swa_output(x, o, gate, p), kk[:, -c:], vv[:, -c:]


def setup_inputs(seed: int = 0) -> dict:
    key = jax.random.key(seed)
    ks = iter(jax.random.split(key, 40))

    def nrm(shape, scale):
        return scale * jax.random.normal(next(ks), shape, jnp.float32)

    swa_cache = min(SWA_WINDOW, PAST_LEN)
    lam_re = -0.5 + nrm((N_EVEN, S5_GROUPS, S5_STATE), 0.01)
    lam_im = jnp.pi * jnp.arange(S5_STATE, dtype=jnp.float32) + nrm((N_EVEN, S5_GROUPS, S5_STATE), 0.01)
    log_dt = jax.random.uniform(next(ks), (N_EVEN, S5_GROUPS), jnp.float32,
                                minval=math.log(S5_DT_MIN), maxval=math.log(S5_DT_MAX))
    return {
        'x_prompt': nrm((BATCH, SEQ, D_MODEL), 1.0),
        'x_sample': nrm((DEC_BATCH, DEC_SEQ, D_MODEL), 1.0),
        'state_s5_re': nrm((N_EVEN, DEC_BATCH, S5_GROUPS, S5_STATE), 0.1),
        'state_s5_im': nrm((N_EVEN, DEC_BATCH, S5_GROUPS, S5_STATE), 0.1),
        'state_gla': nrm((N_EVEN, DEC_BATCH, GLA_HEADS, GLA_HEAD_K, GLA_HEAD_V), 0.5),
        'cache_swa_k': nrm((N_ODD, DEC_BATCH, swa_cache, SWA_KV_HEADS, SWA_HEAD_DIM), 1.0),
        'cache_swa_v': nrm((N_ODD, DEC_BATCH, swa_cache, SWA_KV_HEADS, SWA_HEAD_DIM), 1.0),
        'even_norm_g': 1.0 + nrm((N_EVEN, D_MODEL), 0.01),
        'even_w_in': nrm((N_EVEN, D_MODEL, EVEN_IN), D_MODEL ** -0.5),
        's5_lambda_re': lam_re,
        's5_lambda_im': lam_im,
        's5_log_dt': log_dt,
        's5_b_re': nrm((N_EVEN, S5_GROUPS, S5_STATE, S5_GROUP), (2 * S5_GROUP) ** -0.5),
        's5_b_im': nrm((N_EVEN, S5_GROUPS, S5_STATE, S5_GROUP), (2 * S5_GROUP) ** -0.5),
        's5_c_re': nrm((N_EVEN, S5_GROUPS, S5_GROUP, S5_STATE), (2 * S5_STATE) ** -0.5),
        's5_c_im': nrm((N_EVEN, S5_GROUPS, S5_GROUP, S5_STATE), (2 * S5_STATE) ** -0.5),
        's5_d': nrm((N_EVEN, S5_WIDTH), 1.0),
        's5_w_glu': nrm((N_EVEN, S5_WIDTH, S5_WIDTH), S5_WIDTH ** -0.5),
        's5_b_glu': nrm((N_EVEN, S5_WIDTH), 0.01),
        'gla_w_gate': nrm((N_EVEN, GLA_GATE_RANK, GLA_K_WIDTH), GLA_GATE_RANK ** -0.5),
        'gla_b_gate': nrm((N_EVEN, GLA_K_WIDTH), 0.01),
        'gla_norm_g': 1.0 + nrm((N_EVEN, GLA_HEAD_V), 0.01),
        'even_w_out': nrm((N_EVEN, EVEN_MIX, D_MODEL), EVEN_MIX ** -0.5),
        'odd_norm_g': 1.0 + nrm((N_ODD, D_MODEL), 0.01),
        'odd_w_in': nrm((N_ODD, D_MODEL, ODD_IN), D_MODEL ** -0.5),
        'swa_q_norm_g': 1.0 + nrm((N_ODD, SWA_HEAD_DIM), 0.01),
        'swa_k_norm_g': 1.0 + nrm((N_ODD, SWA_HEAD_DIM), 0.01),
        'swa_sinks': nrm((N_ODD, SWA_HEADS), 0.5),
        'odd_w_out': nrm((N_ODD, SWA_Q_WIDTH, D_MODEL), SWA_Q_WIDTH ** -0.5),
    }


def reference(x_prompt, x_sample, state_s5_re, state_s5_im, state_gla, cache_swa_k, cache_swa_v,
              even_norm_g, even_w_in, s5_lambda_re, s5_lambda_im, s5_log_dt, s5_b_re, s5_b_im,
              s5_c_re, s5_c_im, s5_d, s5_w_glu, s5_b_glu, gla_w_gate, gla_b_gate, gla_norm_g,
              even_w_out, odd_norm_g, odd_w_in, swa_q_norm_g, swa_k_norm_g, swa_sinks, odd_w_out):
    hp, hs = x_prompt, x_sample
    s5r_p, s5i_p, gla_p, swk_p, swv_p = [], [], [], [], []
    s5r_s, s5i_s, gla_s, swk_s, swv_s = [], [], [], [], []
    for layer in range(DEPTH):
        i = layer // 2
        if layer % 2 == 0:
            p = {'norm_g': even_norm_g[i], 'w_in': even_w_in[i], 'lam_re': s5_lambda_re[i],
                 'lam_im': s5_lambda_im[i], 'log_dt': s5_log_dt[i], 'b_re': s5_b_re[i], 'b_im': s5_b_im[i],
                 'c_re': s5_c_re[i], 'c_im': s5_c_im[i], 'd': s5_d[i], 'w_glu': s5_w_glu[i],
                 'b_glu': s5_b_glu[i], 'w_gate': gla_w_gate[i], 'b_gate': gla_b_gate[i],
                 'gla_norm_g': gla_norm_g[i], 'w_out': even_w_out[i]}
            nb = hp.shape[0]
            s5_zero = jnp.zeros((nb, S5_GROUPS, S5_STATE), jnp.complex64)
            gla_zero = jnp.zeros((nb, GLA_HEADS, GLA_HEAD_K, GLA_HEAD_V), jnp.float32)
            hp, s5_last_p, gla_last_p = even_layer(hp, s5_zero, gla_zero, p)
            s5_init = lax.complex(state_s5_re[i].astype(jnp.float32), state_s5_im[i].astype(jnp.float32))
            hs, s5_last_s, gla_last_s = even_layer(hs, s5_init, state_gla[i].astype(jnp.float32), p)
            s5r_p.append(s5_last_p.real)
            s5i_p.append(s5_last_p.imag)
            gla_p.append(gla_last_p)
            s5r_s.append(s5_last_s.real)
            s5i_s.append(s5_last_s.imag)
            gla_s.append(gla_last_s)
        else:
            p = {'norm_g': odd_norm_g[i], 'w_in': odd_w_in[i], 'q_norm_g': swa_q_norm_g[i],
                 'k_norm_g': swa_k_norm_g[i], 'sinks': swa_sinks[i], 'w_out': odd_w_out[i]}
            hp, k_p, v_p = odd_layer_prompt(hp, p)
            hs, k_s, v_s = odd_layer_sample(hs, cache_swa_k[i], cache_swa_v[i], p)
            swk_p.append(k_p)
            swv_p.append(v_p)
            swk_s.append(k_s)
            swv_s.append(v_s)
    return (hp, hs,
            jnp.stack(s5r_p), jnp.stack(s5i_p), jnp.stack(gla_p), jnp.stack(swk_p), jnp.stack(swv_p),
            jnp.stack(s5r_s), jnp.stack(s5i_s), jnp.stack(gla_s), jnp.stack(swk_s), jnp.stack(swv_s))
```

```python
import math
from contextlib import ExitStack
import numpy as np
import ml_dtypes
import concourse.bass as bass
import concourse.mybir as mybir
from concourse.bass_utils import run_bass_kernel_spmd

F32 = mybir.dt.float32
BF16 = mybir.dt.bfloat16
AF = mybir.ActivationFunctionType
ALU = mybir.AluOpType
AX = mybir.AxisListType
EPS = 1e-6
NCORES = 8
SEQ = 8192
NSMP = 16
D = 1024
EIN = 2576
OIN = 2304


class Buf:
    def __init__(self, name):
        self.name = name
        self.writers = []
        self.readers = []
        self.psum = False


class TB:
    def __init__(self, t, name):
        self.t = t
        self.b = Buf(name)


class Tracker:
    def __init__(self, nc):
        self.nc = nc
        self.eng = {'pe': nc.tensor, 'act': nc.scalar, 'dve': nc.vector, 'pool': nc.gpsimd, 'sp': nc.sync}
        self.sem = {}
        self.cnt = {}
        for k in self.eng:
            self.sem[k] = nc.alloc_semaphore(name='s_' + k)
            self.cnt[k] = 0
        self.seen = {k: {} for k in self.eng}
        self.dma_sems = []

    def _wait(self, E, ev):
        kind, key, val = ev
        if kind == 'eng' and key == E and E in ('pe', 'sp'):
            return
        sem = self.sem[key] if kind == 'eng' else key
        sk = (kind, key if kind == 'eng' else id(key))
        if self.seen[E].get(sk, 0) >= val:
            return
        self.seen[E][sk] = val
        self.eng[E].wait_ge(sem, val)

    def _deps(self, E, reads, writes):
        evs = []
        for b in reads:
            evs += b.writers
            if b.psum:
                evs += [ev for ev in b.readers if not (ev[0] == 'eng' and ev[1] == E)]
        for b in writes:
            evs += b.writers + b.readers
        best = {}
        for ev in evs:
            k = (ev[0], ev[1] if ev[0] == 'eng' else id(ev[1]))
            if k not in best or best[k][2] < ev[2]:
                best[k] = ev
        for ev in best.values():
            self._wait(E, ev)

    def _record(self, ev, reads, writes):
        for b in reads:
            b.readers.append(ev)
            if len(b.readers) > 64:
                b.readers = b.readers[-64:] if False else b.readers
        for b in writes:
            b.writers = [ev]
            b.readers = []

    def op(self, E, fn, reads=(), writes=(), signal=True):
        reads = [x.b if hasattr(x, 'b') else x for x in reads]
        writes = [x.b if hasattr(x, 'b') else x for x in writes]
        self._deps(E, reads, writes)
        ins = fn(self.eng[E])
        if signal:
            self.cnt[E] += 1
            ins.then_inc(self.sem[E], 1)
            ev = ('eng', E, self.cnt[E])
        else:
            ev = ('eng', E, self.cnt[E] + 1)
        self._record(ev, reads, writes)
        return ins

    def new_dma_sem(self, name):
        h = [self.nc.alloc_semaphore(name=name), 0]
        self.dma_sems.append(h)
        return h

    def dma(self, Q, semh, out, in_, reads=(), writes=(), serial=True, **kw):
        reads = [x.b if hasattr(x, 'b') else x for x in reads]
        writes = [x.b if hasattr(x, 'b') else x for x in writes]
        self._deps(Q, reads, writes)
        if serial and semh[1] > 0:
            self._wait(Q, ('dma', semh[0], semh[1]))
        semh[1] += 16
        self.eng[Q].dma_start(out=out, in_=in_, **kw).then_inc(semh[0], 16)
        ev = ('dma', semh[0], semh[1])
        self._record(ev, reads, writes)

    def barrier(self):
        for E in self.eng:
            for F in self.eng:
                if F != E and self.cnt[F] > 0:
                    self._wait(E, ('eng', F, self.cnt[F]))
            for s, c in self.dma_sems:
                if c:
                    self._wait(E, ('dma', s, c))

    def finish(self, E='sp'):
        for s, c in self.dma_sems:
            if c:
                self.eng[E].wait_ge(s, c)


def build_program(NPRE=47, NMAIN=16, with_sample=True):
    nc = bass.Bass("TRN2", target_bir_lowering=False)
    TOK = NMAIN * 128

    def din(name, shape, dt=F32):
        return nc.dram_tensor(name, list(shape), dt, kind="ExternalInput").ap()

    def dout(name, shape, dt=F32):
        return nc.dram_tensor(name, list(shape), dt, kind="ExternalOutput").ap()

    x_pre = din("x_pre", [max(NPRE, 1) * 128, D])
    x_main = din("x_main", [TOK + 128, D])
    x_s = din("x_s", [NSMP * 128, D])
    st_s5r = din("st_s5r", [NSMP, 2048])
    st_s5i = din("st_s5i", [NSMP, 2048])
    st_gla = din("st_gla", [NSMP, 4, 64, 128])
    ck_in = din("ck_in", [NSMP, 128, 128])
    cv_in = din("cv_in", [NSMP, 128, 128])
    e_ng = din("e_ng", [1, D]); e_win = din("e_win", [D, EIN])
    lam_re = din("lam_re", [32, 64]); lam_im = din("lam_im", [32, 64]); log_dt = din("log_dt", [1, 32])
    b_re = din("b_re", [32, 64, 16]); b_im = din("b_im", [32, 64, 16])
    c_re = din("c_re", [512, 64]); c_im = din("c_im", [512, 64])
    s5_d = din("s5_d", [512, 1]); w_glu = din("w_glu", [512, 512]); b_glu = din("b_glu", [512, 1])
    w_gate = din("w_gate", [16, 256]); b_gate = din("b_gate", [256, 1]); gla_g = din("gla_g", [128, 1])
    e_wout = din("e_wout", [D, D])
    o_ng = din("o_ng", [1, D]); o_win = din("o_win", [D, OIN]); qg = din("qg", [1, 64]); kg = din("kg", [1, 64])
    sinks = din("sinks", [1, 16]); o_wout = din("o_wout", [D, D])
    rope_p = din("rope_p", [TOK + 128, 64])
    rope_s = din("rope_s", [128, 64])
    vmask_s = din("vmask_s", [1, 128])
    m_gla = din("m_gla", [128, 128])
    m_gla_s = din("m_gla_s", [64, 64])
    m_prev = din("m_prev", [128, 128], BF16)
    m_cur = din("m_cur", [128, 128], BF16)
    m_prev0 = din("m_prev0", [128, 128], BF16)
    m_sc = din("m_sc", [128, 124], BF16)
    m_sn = din("m_sn", [128, 64], BF16)
    identf = din("identf", [128, 128])
    rmask = din("rmask", [1, 128])
    rmask_s = din("rmask_s", [1, 64])
    krev = din("krev", [1, 128])
    seqsel = din("seqsel", [64, NSMP])

    y_p = dout("y_p", [TOK, D]); y_s = dout("y_s", [NSMP * 4, D])
    o_s5r_p = dout("o_s5r_p", [128, 16]); o_s5i_p = dout("o_s5i_p", [128, 16])
    o_gla_p = dout("o_gla_p", [2, 128, 128])
    o_k_p = dout("o_k_p", [128, 128]); o_v_p = dout("o_v_p", [128, 128])
    o_s5r_s = dout("o_s5r_s", [NSMP, 2048]); o_s5i_s = dout("o_s5i_s", [NSMP, 2048])
    o_gla_s = dout("o_gla_s", [NSMP, 4, 64, 128])
    o_k_s = dout("o_k_s", [NSMP, 128, 128]); o_v_s = dout("o_v_s", [NSMP, 128, 128])
    h1_p = nc.dram_tensor("h1_p", [TOK + 128, D], F32, kind="Internal").ap()
    h1_s = nc.dram_tensor("h1_s", [128, D], F32, kind="Internal").ap()
    Bh1p = Buf("h1p"); Bh1s = Buf("h1s"); Bckout = Buf("ckout")

    es = ExitStack()
    with es:
        T = Tracker(nc)
        cnt = [0]

        cur_es = [es]

        def sb(shape, dt=F32, name=None):
            cnt[0] += 1
            nm = (name or "t") + str(cnt[0])
            return TB(cur_es[0].enter_context(nc.sbuf_tensor(nm, list(shape), dt)), nm)

        PS = []
        for i in range(8):
            PS.append(TB(es.enter_context(nc.psum_tensor("ps%d" % i, [128, 512], F32)), "ps%d" % i))
            PS[-1].b.psum = True
        psrr = [0]

        def nps():
            p = PS[psrr[0] % 6]
            psrr[0] += 1
            return p

        cpool = [T.new_dma_sem("cst%d" % i) for i in range(8)]
        cidx = [0]

        def csem():
            cidx[0] += 1
            return cpool[cidx[0] % 8]

        def bs(tb):
            if not hasattr(tb, 'sem'):
                tb.sem = T.new_dma_sem("b_" + tb.b.name)
            return tb.sem
        s_w = T.new_dma_sem("w")
        s_x = T.new_dma_sem("x")
        s_out = T.new_dma_sem("out")
        s_h = T.new_dma_sem("h")
        s_misc = T.new_dma_sem("misc")

        def ld(dst, ap_out, ap_in, q='sp', sem=None):
            T.dma(q, sem or csem(), ap_out, ap_in, writes=[dst])

        def V(E, fn, rd, wr):
            return T.op(E, fn, reads=rd, writes=wr)

        def mm(ps, out, lhsT, rhs, rd, start=True, stop=True, signal=True):
            return T.op('pe', lambda e: e.matmul(out, lhsT=lhsT, rhs=rhs, start=start, stop=stop),
                        reads=rd, writes=[ps], signal=signal)

        def tr(ps, out, in_, ident, rd, signal=True):
            return T.op('pe', lambda e: e.transpose(out, in_, ident), reads=rd, writes=[ps], signal=signal)

        idf = sb([128, 128]); ld(idf, idf.t[:], identf)
        idb = sb([128, 128], BF16); V('dve', lambda e: e.tensor_copy(out=idb.t[:], in_=idf.t[:]), [idf], [idb])
        ones_b = sb([128, 128], BF16); V('dve', lambda e: e.memset(ones_b.t[:], 1.0), [], [ones_b])
        W = sb([128, 8 * EIN + 8 * D + 4 * 512], BF16, "W")
        w_in0 = W.t[:, 0:8 * EIN].rearrange("p (k c) -> p k c", k=8)
        w_out0 = W.t[:, 8 * EIN:8 * EIN + 8 * D].rearrange("p (k c) -> p k c", k=8)
        w_gl = W.t[:, 8 * EIN + 8 * D:].rearrange("p (k c) -> p k c", k=4)
        W2b = Buf("W2"); s_w2 = T.new_dma_sem("w2")
        for k in range(8):
            T.dma('pool', s_w, w_in0[:, k, :], e_win[k * 128:(k + 1) * 128, :], writes=[W.b], serial=False)
        for k in range(8):
            T.dma('pool', s_w2, w_out0[:, k, :], e_wout[k * 128:(k + 1) * 128, :], writes=[W2b], serial=False)
        for k in range(4):
            T.dma('pool', s_w2, w_gl[:, k, :], w_glu[k * 128:(k + 1) * 128, :], writes=[W2b], serial=False)
        g0 = sb([128, D]); ld(g0, g0.t[:], e_ng.partition_broadcast(128))
        dcol = sb([128, 4]); bgl = sb([128, 4])
        for c in range(4):
            ld(dcol, dcol.t[:, c:c + 1], s5_d[c * 128:(c + 1) * 128, :])
            ld(bgl, bgl.t[:, c:c + 1], b_glu[c * 128:(c + 1) * 128, :])
        bglh = sb([128, 4])
        V('dve', lambda e: e.tensor_scalar(out=bglh.t[:], in0=bgl.t[:], scalar1=0.5, scalar2=None, op0=ALU.mult), [bgl], [bglh])
        wg = sb([16, 256]); ld(wg, wg.t[:], w_gate)
        nbg = sb([128, 2])
        for hp in range(2):
            ld(nbg, nbg.t[:, hp:hp + 1], b_gate[hp * 128:(hp + 1) * 128, :])
        V('dve', lambda e: e.tensor_scalar(out=nbg.t[:], in0=nbg.t[:], scalar1=-1.0, scalar2=None, op0=ALU.mult), [nbg], [nbg])
        gng = sb([128, 1]); ld(gng, gng.t[:], gla_g)
        mg = sb([128, 128]); ld(mg, mg.t[:], m_gla)
        rm2 = sb([128, 256])
        for hp in range(2):
            ld(rm2, rm2.t[:, hp * 128:(hp + 1) * 128], rmask.partition_broadcast(128))

        def tt(out, in0, in1, op, rd, wr, E='dve'):
            return V(E, lambda e: e.tensor_tensor(out=out, in0=in0, in1=in1, op=op), rd, wr)

        def ts(out, in0, s1, s2, op0, op1, rd, wr, E='dve'):
            if s2 is None:
                return V(E, lambda e: e.tensor_scalar(out=out, in0=in0, scalar1=s1, scalar2=None, op0=op0), rd, wr)
            return V(E, lambda e: e.tensor_scalar(out=out, in0=in0, scalar1=s1, scalar2=s2, op0=op0, op1=op1), rd, wr)

        def stt(out, in0, scalar, in1, op0, op1, rd, wr, E='dve'):
            return V(E, lambda e: e.scalar_tensor_tensor(out=out, in0=in0, scalar=scalar, in1=in1, op0=op0, op1=op1), rd, wr)

        def act(out, in_, func, rd, wr, **kw):
            return V('act', lambda e: e.activation(out=out, in_=in_, func=func, **kw), rd, wr)

        def cp(out, in_, rd, wr, E='dve'):
            if E == 'act':
                return V(E, lambda e: e.activation(out=out, in_=in_, func=AF.Copy), rd, wr)
            return V(E, lambda e: e.tensor_copy(out=out, in_=in_), rd, wr)

        MUL, ADD, SUB = ALU.mult, ALU.add, ALU.subtract
        PI = math.pi

        esA = ExitStack(); esA.__enter__(); cur_es[0] = esA
        cm = sb([128, 16, 128]); sm = sb([128, 16, 128]); rho = sb([128, 16])
        WT = sb([128, 16, 2, 128], BF16, "WT"); Bpr = sb([128, 16, 32]); Bpi = sb([128, 16, 32])
        L128r = sb([128, 16]); L128i = sb([128, 16])
        BbT = sb([128, 16, 2, 128], BF16, "BbT"); V('pool', lambda e: e.memset(BbT.t[:], 0.0), [], [BbT])
        CT = sb([128, 16, 2, 128], BF16, "CT"); V('pool', lambda e: e.memset(CT.t[:], 0.0), [], [CT])
        esS = ExitStack(); esS.__enter__(); cur_es[0] = esS
        lre = sb([128, 16]); lim = sb([128, 16]); ldt = sb([128, 16])
        for gpar in range(2):
            rows = slice(gpar * 64, (gpar + 1) * 64)
            T.dma('sp', csem(), lre.t[rows, :], lam_re.rearrange("(a b) p -> b p a", b=2)[gpar], writes=[lre.b],
                  allow_slow_non_contiguous=True)
            T.dma('sp', csem(), lim.t[rows, :], lam_im.rearrange("(a b) p -> b p a", b=2)[gpar], writes=[lim.b],
                  allow_slow_non_contiguous=True)
            T.dma('sp', csem(), ldt.t[rows, :],
                  log_dt.rearrange("o (a b) -> o b a", b=2)[:, gpar, :].partition_broadcast(64), writes=[ldt.b],
                  allow_slow_non_contiguous=True)
        dtt = sb([128, 16]); act(dtt.t[:], ldt.t[:], AF.Exp, [ldt], [dtt])
        aa = sb([128, 16]); tt(aa.t[:], lre.t[:], dtt.t[:], MUL, [lre, dtt], [aa])
        th = sb([128, 16]); tt(th.t[:], lim.t[:], dtt.t[:], MUL, [lim, dtt], [th])
        tmp16 = sb([128, 16])

        def wrap(t, n):
            for _ in range(n):
                ts(tmp16.t[:], t.t[:], PI, 2 * PI, ALU.is_gt, MUL, [t], [tmp16])
                tt(t.t[:], t.t[:], tmp16.t[:], SUB, [t, tmp16], [t])
        wrap(th, 5)
        th2 = sb([128, 16]); ts(th2.t[:], th.t[:], PI / 2, None, ADD, None, [th], [th2]); wrap(th2, 1)
        sn = sb([128, 16]); act(sn.t[:], th.t[:], AF.Sin, [th], [sn])
        cs = sb([128, 16]); act(cs.t[:], th2.t[:], AF.Sin, [th2], [cs])
        act(rho.t[:], aa.t[:], AF.Exp, [aa], [rho])
        lbr = sb([128, 16]); tt(lbr.t[:], rho.t[:], cs.t[:], MUL, [rho, cs], [lbr])
        lbi = sb([128, 16]); tt(lbi.t[:], rho.t[:], sn.t[:], MUL, [rho, sn], [lbi])
        nr = sb([128, 16]); ts(nr.t[:], lbr.t[:], -1.0, None, ADD, None, [lbr], [nr])
        den = sb([128, 16]); t2 = sb([128, 16])
        tt(den.t[:], lre.t[:], lre.t[:], MUL, [lre], [den]); tt(t2.t[:], lim.t[:], lim.t[:], MUL, [lim], [t2])
        tt(den.t[:], den.t[:], t2.t[:], ADD, [den, t2], [den])
        V('dve', lambda e: e.reciprocal(out=den.t[:], in_=den.t[:]), [den], [den])
        wr_ = sb([128, 16]); wi_ = sb([128, 16])
        tt(wr_.t[:], nr.t[:], lre.t[:], MUL, [nr, lre], [wr_]); tt(t2.t[:], lbi.t[:], lim.t[:], MUL, [lbi, lim], [t2])
        tt(wr_.t[:], wr_.t[:], t2.t[:], ADD, [wr_, t2], [wr_]); tt(wr_.t[:], wr_.t[:], den.t[:], MUL, [wr_, den], [wr_])
        tt(wi_.t[:], lbi.t[:], lre.t[:], MUL, [lbi, lre], [wi_]); tt(t2.t[:], nr.t[:], lim.t[:], MUL, [nr, lim], [t2])
        tt(wi_.t[:], wi_.t[:], t2.t[:], SUB, [wi_, t2], [wi_]); tt(wi_.t[:], wi_.t[:], den.t[:], MUL, [wi_, den], [wi_])
        cp(cm.t[:, :, 0:1], cs.t[:].unsqueeze(2), [cs], [cm]); cp(sm.t[:, :, 0:1], sn.t[:].unsqueeze(2), [sn], [sm])
        ta = sb([128, 16, 64]); tb = sb([128, 16, 64])
        n = 1
        while n < 128:
            br_ = cm.t[:, :, n - 1:n].to_broadcast([128, 16, n]); bi_ = sm.t[:, :, n - 1:n].to_broadcast([128, 16, n])
            a_r = cm.t[:, :, 0:n]; a_i = sm.t[:, :, 0:n]
            tt(ta.t[:, :, 0:n], a_r, br_, MUL, [cm, sm], [ta]); tt(tb.t[:, :, 0:n], a_i, bi_, MUL, [cm, sm], [tb])
            tt(cm.t[:, :, n:2 * n], ta.t[:, :, 0:n], tb.t[:, :, 0:n], SUB, [ta, tb], [cm])
            tt(ta.t[:, :, 0:n], a_r, bi_, MUL, [cm, sm], [ta]); tt(tb.t[:, :, 0:n], a_i, br_, MUL, [cm, sm], [tb])
            tt(sm.t[:, :, n:2 * n], ta.t[:, :, 0:n], tb.t[:, :, 0:n], ADD, [ta, tb], [sm])
            n *= 2
        blr = sb([128, 16, 16]); bli = sb([128, 16, 16])
        for gpar in range(2):
            rows = slice(gpar * 64, (gpar + 1) * 64)
            ld(blr, blr.t[rows], b_re.rearrange("(a b) p c -> b p a c", b=2)[gpar])
            ld(bli, bli.t[rows], b_im.rearrange("(a b) p c -> b p a c", b=2)[gpar])
        bbr = sb([128, 16, 16]); bbi = sb([128, 16, 16]); t3 = sb([128, 16, 16])
        wrb = wr_.t[:].unsqueeze(2).to_broadcast([128, 16, 16]); wib = wi_.t[:].unsqueeze(2).to_broadcast([128, 16, 16])
        tt(bbr.t[:], blr.t[:], wrb, MUL, [blr, wr_], [bbr]); tt(t3.t[:], bli.t[:], wib, MUL, [bli, wi_], [t3])
        tt(bbr.t[:], bbr.t[:], t3.t[:], SUB, [bbr, t3], [bbr])
        tt(bbi.t[:], bli.t[:], wrb, MUL, [bli, wr_], [bbi]); tt(t3.t[:], blr.t[:], wib, MUL, [blr, wi_], [t3])
        tt(bbi.t[:], bbi.t[:], t3.t[:], ADD, [bbi, t3], [bbi])
        stg = sb([16, 16, 2, 128], BF16)
        s_bbt = T.new_dma_sem("bbt")
        for gpair in range(16):
            for ri, src in ((0, bbr), (1, bbi)):
                p = nps()
                tr(p, p.t[0:16, 0:128], src.t[:, gpair, :], idf.t[:], [src, idf])
                cp(stg.t[:, gpair, ri, :], p.t[0:16, 0:128], [p], [stg])
        for g in range(32):
            gpair, gpar, gpos = g // 2, g % 2, g % 8
            for ri in range(2):
                T.dma('sp', s_bbt, BbT.t[gpos * 16:(gpos + 1) * 16, gpair, ri, gpar * 64:(gpar + 1) * 64],
                      stg.t[:, gpair, ri, gpar * 64:(gpar + 1) * 64], reads=[stg.b], writes=[BbT.b], serial=False)
        cls = [[sb([128, 128]) for _ in range(2)] for _ in range(4)]
        for c4 in range(4):
            for ri, src in ((0, c_re), (1, c_im)):
                cl = cls[c4][ri]
                ld(cl, cl.t[:, 0:64], src[c4 * 128:(c4 + 1) * 128, :]); ld(cl, cl.t[:, 64:128], src[c4 * 128:(c4 + 1) * 128, :])
        for c4 in range(4):
            for ri, src in ((0, c_re), (1, c_im)):
                cl = cls[c4][ri]
                p = nps()
                tr(p, p.t[:, 0:128], cl.t[:], idf.t[:], [cl, idf])
                for gpos in range(8):
                    g = 8 * c4 + gpos; gpar = g % 2
                    rows = slice(gpar * 64, (gpar + 1) * 64); cols = slice(gpos * 16, (gpos + 1) * 16)
                    ts(CT.t[rows, g // 2, ri, cols], p.t[rows, cols], 1.0 if ri == 0 else -1.0, None, MUL, None, [p], [CT])

        kr = sb([128, 128]); ld(kr, kr.t[:], krev.partition_broadcast(128))
        rrev = sb([128, 16, 128]); Wr_ = sb([128, 16, 128]); Wi_ = sb([128, 16, 128]); tq = sb([128, 16, 128])
        for gp_ in range(16):
            act(rrev.t[:, gp_, :], kr.t[:], AF.Exp, [kr, aa], [rrev], scale=aa.t[:, gp_:gp_ + 1])
        e128r = cm.t[:, :, 127:128].to_broadcast([128, 16, 128]); e128i = sm.t[:, :, 127:128].to_broadcast([128, 16, 128])
        tt(Wr_.t[:], cm.t[:], e128r, MUL, [cm], [Wr_]); tt(tq.t[:], sm.t[:], e128i, MUL, [sm], [tq])
        tt(Wr_.t[:], Wr_.t[:], tq.t[:], ADD, [Wr_, tq], [Wr_]); tt(Wr_.t[:], Wr_.t[:], rrev.t[:], MUL, [Wr_, rrev], [Wr_])
        tt(Wi_.t[:], cm.t[:], e128i, MUL, [cm, sm], [Wi_]); tt(tq.t[:], sm.t[:], e128r, MUL, [sm, cm], [tq])
        tt(Wi_.t[:], Wi_.t[:], tq.t[:], SUB, [Wi_, tq], [Wi_]); tt(Wi_.t[:], Wi_.t[:], rrev.t[:], MUL, [Wi_, rrev], [Wi_])
        for gp_ in range(16):
            for ri, src in ((0, Wr_), (1, Wi_)):
                p = nps()
                tr(p, p.t[:, 0:128], src.t[:, gp_, :], idf.t[:], [src, idf])
                cp(WT.t[:, gp_, ri, :], p.t[:, 0:128], [p], [WT], E=('act' if ri else 'dve'))
        V('pool', lambda e: e.memset(Bpr.t[:], 0.0), [], [Bpr]); V('pool', lambda e: e.memset(Bpi.t[:], 0.0), [], [Bpi])
        for half in range(2):
            rows = slice(half * 64, (half + 1) * 64); cols = slice(half * 16, (half + 1) * 16)
            cp(Bpr.t[rows, :, cols], bbr.t[rows, :, :], [bbr], [Bpr]); cp(Bpi.t[rows, :, cols], bbi.t[rows, :, :], [bbi], [Bpi])
        r128 = sb([128, 16]); act(r128.t[:], aa.t[:], AF.Exp, [aa], [r128], scale=128.0)
        tt(L128r.t[:], r128.t[:], cm.t[:, :, 127], MUL, [r128, cm], [L128r]); tt(L128i.t[:], r128.t[:], sm.t[:, :, 127], MUL, [r128, sm], [L128i])
        T.barrier()
        esS.close(); cur_es[0] = esA
        xcr = sb([128, 16]); xci = sb([128, 16])
        Sp = [sb([128, 128], name="S") for _ in range(2)]
        vmask = sb([128, 128]); ld(vmask, vmask.t[:], vmask_s.partition_broadcast(128))
        chain_of = {}
        POOLX = 'dve'

        def run_jobs(jobs, XT):
            active = []
            nxt = 0
            ready = True
            xassign = {}
            xfree = [0, 1, 2]

            def issue_x(i):
                if i >= len(jobs) or i in xassign or not xfree:
                    return
                k = xfree.pop(0); xassign[i] = k
                ap, rows, rd = jobs[i].xsrc
                xb = XT[k]
                T.dma('sp', bs(xb), xb.t[0:rows, :], ap[0:rows, :], reads=rd, writes=[xb.b])
            while nxt < len(jobs) or active:
                free = [s_ for s_ in (0, 1) if all(e[1] != s_ for e in active)]
                if nxt < len(jobs) and ready and free:
                    slot = free[0]
                    vt0 = min([e[2] for e in active], default=0.0)
                    issue_x(nxt)
                    assert nxt in xassign
                    active.append([jobs[nxt](slot, XT[xassign[nxt]]), slot, vt0, nxt]); nxt += 1; ready = False
                    issue_x(nxt)
                ent = min(active, key=lambda e: e[2])
                try:
                    r = next(ent[0])
                    if r == 'A':
                        if ent is active[-1]:
                            ready = True
                        r = 2.4
                    ent[2] += float(r) if r else 1.0
                except StopIteration:
                    if ent is active[-1]:
                        ready = True
                    active.remove(ent)
                    xfree.append(xassign[ent[3]])
                    issue_x(nxt)

        def make_l0(slot):
            xn = sb([128, D], BF16); xnT = sb([128, 8, 128], BF16)
            ssq = sb([128, 1]); rstd = sb([128, 1]); junk = xn
            s_h = T.new_dma_sem("h%d" % slot)
            rr = [0]

            def nps():
                p = PS[4 * slot + 1 + rr[0] % 3]
                rr[0] += 1
                return p
            PFIX = PS[4 * slot]
            uF = sb([128, 4, 128], BF16); sgA = sb([128, 4, 128], BF16); sgB = sb([128, 4, 128], BF16)
            qkF = sb([128, 4, 128]); alF = sb([16, 128]); vT = sb([128, 512], BF16)
            mixT = sb([128, 8, 128], BF16)
            w1 = sb([128, 512]); w2 = sb([128, 512]); mre = sb([128, 512]); mim = sb([128, 512])
            Xrb = sb([128, 4, 128], BF16); Xib = sb([128, 4, 128], BF16)
            yy = w1; y2 = w2; zf = mre; zz = sb([128, 4, 128], BF16)
            class _V:
                def __init__(self, tb, ap):
                    self.t = ap; self.b = tb.b
            spt = _V(w1, w1.t[:, 0:256]); bc = _V(w1, w1.t[:, 256:512]); eq = _V(w2, w2.t[:, 0:256]); ek = _V(w2, w2.t[:, 256:512])
            dtl = _V(mre, mre.t[:, 0:256])
            ech = sb([128, 4]); s1 = sb([128, 16]); s2 = sb([128, 16]); s3 = sb([128, 16]); s4 = sb([128, 16]); uT = sb([128, 512], BF16)
            qdA = sb([128, 256], BF16); qdB = sb([128, 256], BF16); kdA = sb([128, 256], BF16); kdB = sb([128, 256], BF16)
            for z_ in (qdA, qdB, kdA, kdB):
                V('pool', lambda e, z_=z_: e.memset(z_.t[:], 0.0), [], [z_])
            ktl = sb([128, 256], BF16); ktA = sb([128, 256], BF16); ktB = sb([128, 256], BF16)
            V('pool', lambda e: e.memset(ktA.t[:], 0.0), [], [ktA]); V('pool', lambda e: e.memset(ktB.t[:], 0.0), [], [ktB])
            att = sb([128, 512], BF16)
            Sb0 = [sb([128, 128], BF16) for _ in range(2)]
            Sb1 = [sb([128, 128], BF16) for _ in range(2)]
            osq = sb([128, 512], BF16); orst = mim; ob = mre
            hh = (mim, w2)

            def rmsnorm_T(xt, gt):
                act(junk.t[:], xt.t[:], AF.Square, [xt], [junk, ssq], accum_out=ssq.t[:])
                act(rstd.t[:], ssq.t[:], AF.Ln, [ssq], [rstd], scale=1.0 / D, bias=EPS)
                act(rstd.t[:], rstd.t[:], AF.Exp, [rstd], [rstd], scale=-0.5)
                stt(xn.t[:], xt.t[:], rstd.t[:, 0:1], gt.t[:], MUL, MUL, [xt, rstd, gt], [xn])
                p = nps()
                pb = p.t[:].bitcast(BF16)
                for k in range(8):
                    tr(p, pb[:, k * 128:(k + 1) * 128], xn.t[:, k * 128:(k + 1) * 128], idb.t[:], [xn, idb], signal=(k == 7))
                cp(xnT.t[:].rearrange("p k t -> p (k t)"), pb[:, 0:1024], [p], [xnT], E='act')

            def projF(col0, ntile, p):
                for c in range(ntile):
                    for k in range(8):
                        mm(p, p.t[:, c * 128:(c + 1) * 128], w_in0[:, k, col0 + c * 128:col0 + (c + 1) * 128], xnT.t[:, k, :],
                           [W, xnT], start=(k == 0), stop=(k == 7), signal=(k == 7 and c == ntile - 1))

            def layer0_tile(xsrc, h1dst, h1buf, endcol, smp, so=False, carry=None, pre=None, post=None, xtb=None):
                xt = xtb
                xcr, xci, Sp = carry[:3]
                nwp = 4 if smp else 128

                def F3(ap):
                    return ap.rearrange("p (c t) -> p c t", c=4)[:, :, 0:nwp]
                chain = chain_of.setdefault(id(xcr), {'n': 0, 'done': {}})
                my = chain['n']; chain['n'] += 1
                if pre is not None:
                    pre()
                rmsnorm_T(xt, g0)
                yield 3.0
                if so:
                    p = nps()
                    for k in range(8):
                        mm(p, p.t[:, :], xnT.t[:, k, :], w_in0[:, k, 0:512], [W, xnT], start=(k == 0), stop=(k == 7), signal=(k == 7))
                    cp(uT.t[:], p.t[:], [p], [uT])
                else:
                    p = nps(); projF(0, 4, p); cp(uF.t[:, :, 0:nwp], F3(p.t[:]), [p], [uF], E='act')
                yield (2.5 if so else 4.0)
                if not so:
                    p = nps(); projF(512, 4, p)
                    act(F3(w1.t[:]), F3(p.t[:]), AF.Tanh, [p], [w1], scale=0.5); act(F3(w2.t[:]), F3(p.t[:]), AF.Identity, [p], [w2], scale=0.125)
                    stt(sgA.t[:, :, 0:nwp], F3(w1.t[:]), 1.0, F3(w2.t[:]), ADD, MUL, [w1, w2], [sgA])
                    yield 4.5
                    p = nps(); projF(1024, 4, p); cp(qkF.t[:].rearrange("p c t -> p (c t)"), p.t[:], [p], [qkF], E='act')
                else:
                    p = nps(); projF(1280, 2, p); cp(qkF.t[:, 2:4, :].rearrange("p c t -> p (c t)"), p.t[:, 0:256], [p], [qkF])
                p = nps()
                for k in range(8):
                    mm(p, p.t[0:16, 0:128], w_in0[:, k, 2048:2064], xnT.t[:, k, :], [W, xnT], start=(k == 0), stop=(k == 7), signal=(k == 7))
                cp(alF.t[:], p.t[0:16, 0:128], [p], [alF], E='act')
                yield (3.2 if so else 5.2)
                if not so:
                    p = nps(); projF(2064, 4, p)
                    act(w1.t[:], p.t[:], AF.Tanh, [p], [w1], scale=0.5); act(w2.t[:], p.t[:], AF.Identity, [p], [w2], scale=0.5)
                    stt(sgB.t[:].rearrange("p c t -> p (c t)"), w1.t[:], 1.0, w2.t[:], ADD, MUL, [w1, w2], [sgB])
                    yield 4.5
                p = nps()
                for k in range(8):
                    mm(p, p.t[:, :], xnT.t[:, k, :], w_in0[:, k, 1536:2048], [W, xnT], start=(k == 0), stop=(k == 7), signal=(k == 7))
                cp(vT.t[:], p.t[:], [p], [vT], E='act')
                yield 'A'
                py = None if so else PFIX
                if so:
                    while chain['done'].get(('s5', 3), 0) < my:
                        yield 0
                    pR = nps(); pI = nps()
                    for gp_ in range(16):
                        for ri, pp in ((0, pR), (1, pI)):
                            mm(pp, pp.t[:, gp_ * 32:(gp_ + 1) * 32], WT.t[:, gp_, ri, :], uT.t[:, gp_ * 32:(gp_ + 1) * 32], [WT, uT],
                               signal=(gp_ == 15))
                    yield 3.4
                    bpr = Bpr.t[:].rearrange("p g c -> p (g c)"); bpi = Bpi.t[:].rearrange("p g c -> p (g c)")
                    w1g = w1.t[:].rearrange("p (g c) -> p g c", g=16)
                    tt(w1.t[:], pR.t[:], bpr, MUL, [pR, Bpr], [w1]); tt(w2.t[:], pI.t[:], bpi, MUL, [pI, Bpi], [w2])
                    tt(w1.t[:], w1.t[:], w2.t[:], SUB, [w1, w2], [w1])
                    V('dve', lambda e: e.tensor_reduce(out=s1.t[:], in_=w1g, axis=AX.X, op=ADD), [w1], [s1])
                    tt(w1.t[:], pR.t[:], bpi, MUL, [pR, Bpi], [w1]); tt(w2.t[:], pI.t[:], bpr, MUL, [pI, Bpr], [w2])
                    tt(w1.t[:], w1.t[:], w2.t[:], ADD, [w1, w2], [w1])
                    V('dve', lambda e: e.tensor_reduce(out=s2.t[:], in_=w1g, axis=AX.X, op=ADD), [w1], [s2])
                    tt(s3.t[:], L128r.t[:], xcr.t[:], MUL, [L128r, xcr], [s3]); tt(s4.t[:], L128i.t[:], xci.t[:], MUL, [L128i, xci], [s4])
                    tt(s3.t[:], s3.t[:], s4.t[:], SUB, [s3, s4], [s3]); tt(s1.t[:], s1.t[:], s3.t[:], ADD, [s1, s3], [s1])
                    tt(s3.t[:], L128r.t[:], xci.t[:], MUL, [L128r, xci], [s3]); tt(s4.t[:], L128i.t[:], xcr.t[:], MUL, [L128i, xcr], [s4])
                    tt(s3.t[:], s3.t[:], s4.t[:], ADD, [s3, s4], [s3]); tt(xci.t[:], s2.t[:], s3.t[:], ADD, [s2, s3], [xci])
                    cp(xcr.t[:], s1.t[:], [s1], [xcr])
                    for c4 in range(4):
                        chain['done'][('s5', c4)] = my + 1
                    yield 6.5
                nw = 4 if smp else 128

                def emit_BU(c4):
                    pr = nps(); pi = nps()
                    for gq in range(4):
                        for ri, pp in ((0, pr), (1, pi)):
                            mm(pp, pp.t[:, gq * 128:gq * 128 + nw], BbT.t[:, 4 * c4 + gq, ri, :], uF.t[:, c4, 0:nw], [BbT, uF],
                               signal=(gq == 3))
                    return pr, pi
                bu_next = None if so else emit_BU(0)
                for c4 in (() if so else range(4)):
                    while chain['done'].get(('s5', c4), 0) < my:
                        yield 0
                    pr, pi = bu_next
                    gs = slice(4 * c4, 4 * c4 + 4)
                    cmv = cm.t[:, gs, 0:nw]; smv = sm.t[:, gs, 0:nw]
                    prv = pr.t[:].rearrange("p (g t) -> p g t", g=4)[:, :, 0:nw]; piv = pi.t[:].rearrange("p (g t) -> p g t", g=4)[:, :, 0:nw]
                    w1v = w1.t[:].rearrange("p (g t) -> p g t", g=4)[:, :, 0:nw]; w2v = w2.t[:].rearrange("p (g t) -> p g t", g=4)[:, :, 0:nw]
                    mrv = mre.t[:].rearrange("p (g t) -> p g t", g=4)[:, :, 0:nw]; miv = mim.t[:].rearrange("p (g t) -> p g t", g=4)[:, :, 0:nw]
                    tt(w1v, prv, cmv, MUL, [pr, cm], [w1]); tt(w2v, piv, smv, MUL, [pi, sm], [w2])
                    tt(mrv, w1v, w2v, ADD, [w1, w2], [mre])
                    tt(w1v, piv, cmv, MUL, [pi, cm], [w1]); tt(w2v, prv, smv, MUL, [pr, sm], [w2])
                    tt(miv, w1v, w2v, SUB, [w1, w2], [mim])
                    yield (1.0 if smp else 4.2)
                    for gq in range(4):
                        gp_ = 4 * c4 + gq
                        V('dve', lambda e, gq=gq, gp_=gp_: e.tensor_tensor_scan(
                            out=mrv[:, gq, :], data0=rho.t[:, gp_:gp_ + 1].to_broadcast([128, nw]), data1=mrv[:, gq, :], initial=xcr.t[:, gp_:gp_ + 1],
                            op0=MUL, op1=ADD), [mre, rho, xcr], [mre])
                        V('dve', lambda e, gq=gq, gp_=gp_: e.tensor_tensor_scan(
                            out=miv[:, gq, :], data0=rho.t[:, gp_:gp_ + 1].to_broadcast([128, nw]), data1=miv[:, gq, :], initial=xci.t[:, gp_:gp_ + 1],
                            op0=MUL, op1=ADD), [mim, rho, xci], [mim])
                    yield (1.2 if smp else 3.0)
                    if so:
                        e_ = endcol
                        s1v = s1.t[:].unsqueeze(2); s2v = s2.t[:].unsqueeze(2)
                        tt(s1v, mrv[:, :, e_:e_ + 1], cmv[:, :, e_:e_ + 1], MUL, [mre, cm], [s1]); tt(s2v, miv[:, :, e_:e_ + 1], smv[:, :, e_:e_ + 1], MUL, [mim, sm], [s2])
                        tt(xcr.t[:, gs], s1.t[:], s2.t[:], SUB, [s1, s2], [xcr])
                        tt(s1v, miv[:, :, e_:e_ + 1], cmv[:, :, e_:e_ + 1], MUL, [mim, cm], [s1]); tt(s2v, mrv[:, :, e_:e_ + 1], smv[:, :, e_:e_ + 1], MUL, [mre, sm], [s2])
                        tt(xci.t[:, gs], s1.t[:], s2.t[:], ADD, [s1, s2], [xci])
                        chain['done'][('s5', c4)] = my + 1
                        yield 0
                        continue
                    tt(w1v, mrv, cmv, MUL, [mre, cm], [w1]); tt(w2v, miv, smv, MUL, [mim, sm], [w2], E=POOLX)
                    tt(Xrb.t[:, :, 0:nw], w1v, w2v, SUB, [w1, w2], [Xrb])
                    tt(xcr.t[:, gs], w1v[:, :, endcol], w2v[:, :, endcol], SUB, [w1, w2], [xcr])
                    tt(w1v, miv, cmv, MUL, [mim, cm], [w1]); tt(w2v, mrv, smv, MUL, [mre, sm], [w2], E=POOLX)
                    tt(Xib.t[:, :, 0:nw], w1v, w2v, ADD, [w1, w2], [Xib])
                    tt(xci.t[:, gs], w1v[:, :, endcol], w2v[:, :, endcol], ADD, [w1, w2], [xci])
                    yield (1.2 if smp else 4.5)
                    if c4 < 3:
                        bu_next = emit_BU(c4 + 1)
                    i = 0
                    for gq in range(4):
                        for ri, xb_ in ((0, Xrb), (1, Xib)):
                            mm(py, py.t[:, c4 * 128:c4 * 128 + nw], CT.t[:, 4 * c4 + gq, ri, :], xb_.t[:, gq, 0:nw], [CT, xb_],
                               start=(i == 0), stop=(i == 7), signal=(i == 7))
                            i += 1
                    chain['done'][('s5', c4)] = my + 1
                    yield 3.4
                if not so:
                  yv = F3(yy.t[:]); y2v = F3(y2.t[:]); zfv = F3(zf.t[:])
                  for c4 in range(4):
                      stt(yv[:, c4, :], uF.t[:, c4, 0:nwp], dcol.t[:, c4:c4 + 1], py.t[:, c4 * 128:c4 * 128 + nwp], MUL, ADD, [uF, dcol, py], [yy])
                  tt(y2v, yv, yv, MUL, [yy], [y2])
                  ts(y2v, y2v, 0.044715, 1.0, MUL, ADD, [y2], [y2])
                  tt(y2v, y2v, yv, MUL, [y2, yy], [y2])
                  yield 0
                  act(y2v, y2v, AF.Tanh, [y2], [y2], scale=0.7978845608028654)
                  yield 0
                  stt(zfv, y2v, 1.0, yv, ADD, MUL, [y2, yy], [zf])
                  cp(zz.t[:, :, 0:nwp], zfv, [zf], [zz], E='act')
                  yield 0
                  p = nps()
                  for c in range(4):
                      for k in range(4):
                          mm(p, p.t[:, c * 128:c * 128 + nwp], w_gl[:, k, c * 128:(c + 1) * 128], zz.t[:, k, 0:nwp], [W2b, zz],
                             start=(k == 0), stop=(k == 3), signal=(k == 3 and c == 3))
                  for c in range(4):
                      act(y2v[:, c, :], p.t[:, c * 128:c * 128 + nwp], AF.Tanh, [p, bglh], [y2], scale=0.25, bias=bglh.t[:, c:c + 1])
                  yield 0
                  stt(zfv, y2v, 1.0, zfv, ADD, MUL, [y2, zf], [zf])
                  tt(mixT.t[:, 0:4, 0:nwp], zfv, sgA.t[:, :, 0:nwp], MUL, [zf, sgA], [mixT])
                if not so:
                    yield 0
                for hp in range(2):
                    while chain['done'].get(('gla', hp), 0) < my:
                        yield 0
                p = nps()
                for hp in range(2):
                    mm(p, p.t[:, hp * 128:(hp + 1) * 128], wg.t[:, hp * 128:(hp + 1) * 128], alF.t[:], [wg, alF])
                for hp in range(2):
                    act(spt.t[:, hp * 128:(hp + 1) * 128], p.t[:, hp * 128:(hp + 1) * 128], AF.Exp, [p, nbg], [spt], scale=-1.0, bias=nbg.t[:, hp:hp + 1])
                act(spt.t[:], spt.t[:], AF.Ln, [spt], [spt], bias=1.0)
                yield 1.5
                if smp:
                    tt(spt.t[:].rearrange("p (h t) -> p h t", h=2), spt.t[:].rearrange("p (h t) -> p h t", h=2),
                       vmask.t[:].unsqueeze(1).to_broadcast([128, 2, 128]), MUL, [spt, vmask], [spt])
                V('dve', lambda e: e.tensor_tensor_scan(out=bc.t[:], data0=rm2.t[:], data1=spt.t[:], initial=0.0, op0=MUL, op1=ADD),
                  [rm2, spt], [bc])
                yield 1.0
                bcv = bc.t[:].rearrange("p (c t) -> p c t", c=4)
                if not so:
                    act(eq.t[:], bc.t[:], AF.Exp, [bc], [eq], scale=-1.0 / 16)
                    act(ek.t[:], bc.t[:], AF.Exp, [bc], [ek], scale=1.0 / 16)
                tt(dtl.t[:].rearrange("p (c t) -> p c t", c=4), bcv[:, :, 63:64].to_broadcast([128, 4, 64]), bcv, SUB, [bc], [dtl])
                act(dtl.t[:], dtl.t[:], AF.Exp, [dtl], [dtl], scale=-1.0 / 16)
                act(ech.t[:], bcv[:, :, 63], AF.Exp, [bc], [ech], scale=-1.0 / 16)
                yield 1.5
                qv = qkF.t[:, 0:2, :].rearrange("p h t -> p (h t)"); kv = qkF.t[:, 2:4, :].rearrange("p h t -> p (h t)")
                if not so:
                    for half, (qd_, kd_) in enumerate(((qdA, kdA), (qdB, kdB))):
                        rows = slice(half * 64, (half + 1) * 64)
                        stt(qd_.t[rows, :], qv[rows, :], 0.125, eq.t[rows, :], MUL, MUL, [qkF, eq], [qd_])
                        tt(kd_.t[rows, :], kv[rows, :], ek.t[rows, :], MUL, [qkF, ek], [kd_])
                tt(ktl.t[:], kv, dtl.t[:], MUL, [qkF, dtl], [ktl])
                yield 1.5
                p = nps(); pb = p.t[:].bitcast(BF16)
                for hp in range(2):
                    tr(p, pb[:, hp * 128:(hp + 1) * 128], ktl.t[:, hp * 128:(hp + 1) * 128], idb.t[:], [ktl, idb], signal=(hp == 1))
                cp(ktA.t[0:64, :], pb[0:64, 0:256], [p], [ktA]); cp(ktB.t[64:128, :], pb[64:128, 0:256], [p], [ktB], E='act')
                yield 1.0
                if not so:
                    for hp in range(2):
                        cp(Sb0[hp].t[:], Sp[hp].t[:], [Sp[hp]], [Sb0[hp]], E='act')
                for c, kt_ in ((0, ktA), (1, ktB)):
                    pa = nps()
                    for h in range(4):
                        mm(pa, pa.t[:, h * 128:(h + 1) * 128], kt_.t[:, (h // 2) * 128:(h // 2 + 1) * 128], vT.t[:, h * 128:(h + 1) * 128],
                           [kt_, vT], signal=(h == 3))
                    for h in range(4):
                        hp, h2 = h // 2, h % 2
                        rows = slice(h2 * 64, (h2 + 1) * 64); S = Sp[hp]
                        stt(S.t[rows, :], S.t[rows, :], ech.t[rows, 2 * hp + c:2 * hp + c + 1], pa.t[rows, h * 128:(h + 1) * 128], MUL, ADD,
                            [S, ech, pa], [S])
                    if c == 0 and not so:
                        for hp in range(2):
                            cp(Sb1[hp].t[:], Sp[hp].t[:], [Sp[hp]], [Sb1[hp]], E='act')
                    yield 1.2
                for hp in range(2):
                    chain['done'][('gla', hp)] = my + 1
                if not so:
                    p = nps()
                    for h in range(4):
                        hp, h2 = h // 2, h % 2
                        kd_ = (kdA, kdB)[h2]; qd_ = (qdA, qdB)[h2]
                        mm(p, p.t[:, h * 128:(h + 1) * 128], kd_.t[:, hp * 128:(hp + 1) * 128], qd_.t[:, hp * 128:(hp + 1) * 128], [kd_, qd_],
                           signal=(h == 3))
                    yield 1.0
                    tt(att.t[:].rearrange("p (h t) -> p h t", h=4), p.t[:].rearrange("p (h t) -> p h t", h=4),
                       mg.t[:].unsqueeze(1).to_broadcast([128, 4, 128]), MUL, [p, mg], [att])
                    yield 1.0
                    po = nps()
                    for h in range(4):
                        hp, h2 = h // 2, h % 2
                        qd_ = (qdA, qdB)[h2]
                        c0 = h * 128
                        mm(po, po.t[:, c0:c0 + 128], vT.t[:, h * 128:(h + 1) * 128], att.t[:, c0:c0 + 128], [vT, att], start=True, stop=False, signal=False)
                        mm(po, po.t[:, c0:c0 + 64], Sb0[hp].t[:], qd_.t[:, hp * 128:hp * 128 + 64], [Sb0[hp], qd_], start=False, stop=False, signal=False)
                        mm(po, po.t[:, c0 + 64:c0 + 128], Sb1[hp].t[:], qd_.t[:, hp * 128 + 64:hp * 128 + 128], [Sb1[hp], qd_], start=False, stop=True,
                           signal=(h == 3))
                    yield 1.5
                    act(osq.t[:], po.t[:], AF.Square, [po], [osq])
                    yield 0.8
                    pn = nps()
                    mm(pn, pn.t[:, :], ones_b.t[:], osq.t[:], [ones_b, osq])
                    yield 0.6
                    act(orst.t[:], pn.t[:], AF.Ln, [pn], [orst], scale=1.0 / 128, bias=EPS)
                    act(orst.t[:], orst.t[:], AF.Exp, [orst], [orst], scale=-0.5)
                    yield 1.2
                    stt(ob.t[:], po.t[:], gng.t[:, 0:1], orst.t[:], MUL, MUL, [po, gng, orst], [ob])
                    tt(mixT.t[:, 4:8, :].rearrange("p h t -> p (h t)"), ob.t[:], sgB.t[:].rearrange("p h t -> p (h t)"), MUL, [ob, sgB], [mixT])
                    yield 1.5
                for half in (() if so else range(2)):
                    p = nps()
                    for k in range(8):
                        mm(p, p.t[:, :], mixT.t[:, k, :], w_out0[:, k, half * 512:(half + 1) * 512], [mixT, W2b],
                           start=(k == 0), stop=(k == 7), signal=(k == 7))
                    tt(hh[half].t[:], p.t[:], xt.t[:, half * 512:(half + 1) * 512], ADD, [p, xt], [hh[half]])
                    nr = 4 if smp else 128
                    T.dma('pool', s_h, h1dst[0:nr, half * 512:(half + 1) * 512], hh[half].t[0:nr, :], reads=[hh[half].b], writes=[h1buf])
                if post is not None:
                    post()
            return layer0_tile

        XT_A = [sb([128, D], name="xtA") for _ in range(3)]
        L0 = [make_l0(0), make_l0(1)]
        for z_ in (xcr, xci):
            V('dve', lambda e, z_=z_: e.memset(z_.t[:], 0.0), [], [z_])
        for hp in range(2):
            V('dve', lambda e, hp=hp: e.memset(Sp[hp].t[:], 0.0), [], [Sp[hp]])
        XS = [sb([128, 16, NSMP]) for _ in range(2)]; XO = XS
        stin = sb([NSMP, 512]); stout = stin
        scar = [(sb([128, 16]), sb([128, 16]), [sb([128, 128]), sb([128, 128])]) for _ in range(2)]
        srr = [0]
        scar_busy = [False, False]

        def sps():
            srr[0] += 1
            return PS[1 + srr[0] % 3]
        if with_sample:
            for ri, src in ((0, st_s5r), (1, st_s5i)):
                for q4 in range(4):
                    T.dma('sp', bs(stin), stin.t[:], src[:, q4 * 512:(q4 + 1) * 512], writes=[stin.b])
                    for gg in range(4):
                        gpair = 4 * q4 + gg
                        p = sps()
                        tr(p, p.t[:, 0:NSMP], stin.t[:, gg * 128:(gg + 1) * 128], idf.t[0:NSMP, 0:NSMP], [stin, idf])
                        cp(XS[ri].t[:, gpair, :], p.t[:, 0:NSMP], [p], [XS[ri]])
        pcarry = (xcr, xci, Sp)
        jobs = []
        for t in range(NPRE):
            f = lambda slot, xtb, t=t: L0[slot](x_pre[t * 128:(t + 1) * 128, :], None, None, 127, False, so=True, carry=pcarry, xtb=xtb)
            f.xsrc = (x_pre[t * 128:(t + 1) * 128, :], 128, []); jobs.append(f)

        def post_prompt():
            T.dma('sp', bs(xcr), o_s5r_p, xcr.t[:], reads=[xcr.b]); T.dma('sp', bs(xci), o_s5i_p, xci.t[:], reads=[xci.b])
            for hp in range(2):
                T.dma('sp', bs(Sp[hp]), o_gla_p[hp], Sp[hp].t[:], reads=[Sp[hp].b])
        mjobs = []
        for t in range(NMAIN + 1):
            f = lambda slot, xtb, t=t: L0[slot](x_main[t * 128:(t + 1) * 128, :], h1_p[t * 128:(t + 1) * 128, :], Bh1p, 127, False,
                                                carry=pcarry, post=(post_prompt if t == NMAIN else None), xtb=xtb)
            f.xsrc = (x_main[t * 128:(t + 1) * 128, :], 128, []); mjobs.append(f)
        sjobs = []
        if with_sample:
            for s in range(NSMP):
                def pre_s(s=s):
                    assert not scar_busy[s % 2], "sample carry buffers still in use"
                    scar_busy[s % 2] = True
                    cr, ci, Ss = scar[s % 2]
                    cp(cr.t[:], XS[0].t[:, :, s], [XS[0]], [cr]); cp(ci.t[:], XS[1].t[:, :, s], [XS[1]], [ci])
                    for hp in range(2):
                        T.dma('sp', bs(Ss[hp]), Ss[hp].t[:], st_gla[s, 2 * hp:2 * hp + 2].rearrange("h k v -> (h k) v"), writes=[Ss[hp].b])

                def post_s(s=s):
                    scar_busy[s % 2] = False
                    cr, ci, Ss = scar[s % 2]
                    cp(XO[0].t[:, :, s], cr.t[:], [cr], [XO[0]]); cp(XO[1].t[:, :, s], ci.t[:], [ci], [XO[1]])
                    for hp in range(2):
                        T.dma('sp', bs(Ss[hp]), o_gla_s[s, 2 * hp:2 * hp + 2].rearrange("h k v -> (h k) v"), Ss[hp].t[:], reads=[Ss[hp].b])
                f = lambda slot, xtb, s=s, pre_s=pre_s, post_s=post_s: L0[slot](
                    x_s[s * 128:(s + 1) * 128, :], h1_s[4 * s:4 * s + 4, :], Bh1s, 3, True,
                    carry=scar[s % 2], pre=pre_s, post=post_s, xtb=xtb)
                f.xsrc = (x_s[s * 128:(s + 1) * 128, :], 128, []); sjobs.append(f)
        if len(jobs) % 2:
            jobs.append(mjobs.pop(0))
        while mjobs or sjobs:
            if mjobs:
                jobs.append(mjobs.pop(0))
            if sjobs:
                jobs.append(sjobs.pop(0))
        run_jobs(jobs, XT_A)
        if with_sample:
            for ri, dst in ((0, o_s5r_s), (1, o_s5i_s)):
                for q4 in range(4):
                    for gg in range(4):
                        gpair = 4 * q4 + gg
                        p = sps()
                        tr(p, p.t[0:NSMP, 0:128], XO[ri].t[:, gpair, :], idf.t[:], [XO[ri], idf])
                        cp(stout.t[:, gg * 128:(gg + 1) * 128], p.t[0:NSMP, 0:128], [p], [stout])
                    T.dma('sp', bs(stout), dst[:, q4 * 512:(q4 + 1) * 512], stout.t[:], reads=[stout.b])

        T.barrier()
        esA.close(); cur_es[0] = es
        w_in1 = W.t[:, 0:8 * OIN].rearrange("p (k c) -> p k c", k=8)
        w_out1 = W.t[:, 8 * OIN:8 * OIN + 8 * D].rearrange("p (k c) -> p k c", k=8)
        for k in range(8):
            T.dma('pool', s_w, w_in1[:, k, :], o_win[k * 128:(k + 1) * 128, :], writes=[W.b, W2b], serial=False)
        for k in range(8):
            T.dma('pool', s_w, w_out1[:, k, :], o_wout[k * 128:(k + 1) * 128, :], writes=[W.b, W2b], serial=False)
        ld(g0, g0.t[:], o_ng.partition_broadcast(128)); g1 = g0
        gq_b = sb([128, 64]); ld(gq_b, gq_b.t[:], qg.partition_broadcast(128))
        gk_b = sb([128, 64]); ld(gk_b, gk_b.t[:], kg.partition_broadcast(128))
        esk = sb([128, 16]); ld(esk, esk.t[:], sinks.partition_broadcast(128))
        act(esk.t[:], esk.t[:], AF.Exp, [esk], [esk])
        mk4 = {}
        for nm, src in (("prev", m_prev), ("cur", m_cur), ("prev0", m_prev0)):
            mt = sb([128, 4, 128], BF16)
            for j in range(4):
                ld(mt, mt.t[:, j, :], src)
            mk4[nm] = mt
        kTs = [sb([128, 128], BF16) for _ in range(3)]
        vas = [sb([128, 2, 68], BF16) for _ in range(3)]
        for va in vas:
            V('pool', lambda e, va=va: e.memset(va.t[:], 1.0), [], [va])

        ckT_all = sb([128, NSMP, 128], BF16); cva_all = sb([128, NSMP, 2, 68], BF16)
        V('pool', lambda e: e.memset(cva_all.t[:], 1.0), [], [cva_all])
        Bm = sb([128, 124], BF16); ld(Bm, Bm.t[:], m_sc)
        msn = sb([128, 64], BF16); ld(msn, msn.t[:], m_sn)

        class _V:
            def __init__(self, tb, ap):
                self.t = ap; self.b = tb.b

        XT_B = [sb([128, D], name="xtB") for _ in range(3)]
        for xb_ in XT_B:
            V('pool', lambda e, xb_=xb_: e.memset(xb_.t[:], 0.0), [], [xb_])

        def make_l1(slot):
            xn = sb([128, D], BF16); xnT = sb([128, 8, 128], BF16)
            ssq = sb([128, 1]); rstd = sb([128, 1]); junk = xn
            rr = [0]

            def nps():
                p = PS[4 * slot + 1 + rr[0] % 2]
                rr[0] += 1
                return p
            PACC = (PS[4 * slot], PS[4 * slot + 3])
            qk = sb([128, 18, 64]); sq1 = sb([128, 18, 64]); ss18 = sb([128, 18]); qr = sb([128, 18, 64])
            V('dve', lambda e: e.memset(qk.t[:], 0.0), [], [qk])
            ra = sb([128, 18, 32]); rb = sb([128, 18, 32]); rp = sb([128, 64])
            qp = sb([128, 16, 128], BF16); V('pool', lambda e: e.memset(qp.t[:], 0.0), [], [qp])
            kb16 = sb([128, 128], BF16); qpT = sb([128, 16, 128], BF16)
            kfp = sb([128, 128]); vfp = sb([128, 128]); sg1 = sb([128, 1024], BF16)
            pex = [sb([128, 512], BF16) for _ in range(2)]
            dn = sb([128, 4]); o4 = sb([128, 4, 64]); og = sb([128, 1024], BF16); ogT = sb([128, 8, 128], BF16)
            tg1 = sb([128, 512]); tg2 = sb([128, 512])
            yt = sb([128, D])
            ckls = [sb([128, 128]) for _ in range(4)]; cvls = [sb([128, 128]) for _ in range(4)]; ckbs = [sb([128, 128], BF16) for _ in range(4)]
            ckl = ckls[0]; cvl = cvls[0]; ckb = ckbs[0]
            skT = sb([128, 128], BF16); sva = sb([128, 2, 68], BF16); ckT = sb([128, 128], BF16); cva = sb([128, 2, 68], BF16)
            V('pool', lambda e: e.memset(sva.t[:], 1.0), [], [sva]); V('pool', lambda e: e.memset(cva.t[:], 1.0), [], [cva])

            def rmsnorm_T(xt, gt):
                act(junk.t[:], xt.t[:], AF.Square, [xt], [junk, ssq], accum_out=ssq.t[:])
                act(rstd.t[:], ssq.t[:], AF.Ln, [ssq], [rstd], scale=1.0 / D, bias=EPS)
                act(rstd.t[:], rstd.t[:], AF.Exp, [rstd], [rstd], scale=-0.5)
                stt(xn.t[:], xt.t[:], rstd.t[:, 0:1], gt.t[:], MUL, MUL, [xt, rstd, gt], [xn])
                p = nps()
                pb = p.t[:].bitcast(BF16)
                for k in range(8):
                    tr(p, pb[:, k * 128:(k + 1) * 128], xn.t[:, k * 128:(k + 1) * 128], idb.t[:], [xn, idb], signal=(k == 7))
                cp(xnT.t[:].rearrange("p k t -> p (k t)"), pb[:, 0:1024], [p], [xnT], E='act')

            def layer1_tile(hsrc, hbuf, rope_src, prev, cur, ydst, yrows, kvonly=False, pmask='prev', smp=None, xtb=None):
                xt = xtb
                if smp is not None and smp != 'packed':
                    s = smp
                    prev = (ckT, cva); cur = (skT, sva)
                    T.dma('sp', bs(ckl), ckl.t[:], ck_in[s], writes=[ckl.b]); T.dma('sp', bs(cvl), cvl.t[:], cv_in[s], writes=[cvl.b])
                    cp(ckb.t[:], ckl.t[:], [ckl], [ckb], E='act')
                    p = nps(); pb = p.t[:].bitcast(BF16)
                    tr(p, pb[:, 0:128], ckb.t[:], idb.t[:], [ckb, idb])
                    cp(ckT.t[:], pb[:, 0:128], [p], [ckT])
                    cp(cva.t[:, :, 0:64], cvl.t[:].rearrange("p (h d) -> p h d", h=2), [cvl], [cva])
                packed = (smp == 'packed')
                if packed:
                    cur = (skT, sva)
                    for s in range(NSMP):
                        ckl = ckls[s % 4]; cvl = cvls[s % 4]; ckb = ckbs[s % 4]
                        T.dma('sp', bs(ckl), ckl.t[:], ck_in[s], writes=[ckl.b]); T.dma('sp', bs(cvl), cvl.t[:], cv_in[s], writes=[cvl.b])
                        cp(ckb.t[:], ckl.t[:], [ckl], [ckb], E='act')
                        p = nps(); pb = p.t[:].bitcast(BF16)
                        tr(p, pb[:, 0:128], ckb.t[:], idb.t[:], [ckb, idb])
                        cp(ckT_all.t[:, s, :], pb[:, 0:128], [p], [ckT_all])
                        cp(cva_all.t[:, s, :, 0:64], cvl.t[:].rearrange("p (h d) -> p h d", h=2), [cvl], [cva_all])
                        T.dma('sp', s_out, o_k_s[s, 0:124, :], ck_in[s, 4:128, :], serial=False)
                        T.dma('sp', s_out, o_v_s[s, 0:124, :], cv_in[s, 4:128, :], serial=False)
                        if s % 4 == 3:
                            yield 4.0
                kT_c, va_c = cur
                rmsnorm_T(xt, g1)
                T.dma('sp', bs(rp), rp.t[:], rope_src, writes=[rp.b])
                yield 3.0
                for half in (() if kvonly else range(2)):
                    pq = nps()
                    for k in range(8):
                        mm(pq, pq.t[:, :], xnT.t[:, k, :], w_in1[:, k, half * 512:(half + 1) * 512], [xnT, W],
                           start=(k == 0), stop=(k == 7), signal=(k == 7))
                    cp(qk.t[:, 8 * half:8 * half + 8, :].rearrange("p h d -> p (h d)"), pq.t[:], [pq], [qk], E=('dve' if half else 'act'))
                    yield 2.3
                pkv = nps()
                for k in range(8):
                    mm(pkv, pkv.t[:, :], xnT.t[:, k, :], w_in1[:, k, 1024:1536], [xnT, W], start=(k == 0), stop=(k == 7), signal=(k == 7))
                cp(qk.t[:, 16:18, :].rearrange("p h d -> p (h d)"), pkv.t[:, 0:128], [pkv], [qk])
                cp(vfp.t[:], pkv.t[:, 128:256], [pkv], [vfp], E='act')
                for h_ in range(2):
                    cp(va_c.t[:, h_, 0:64], vfp.t[:, h_ * 64:(h_ + 1) * 64], [vfp], [va_c])
                yield 2.3
                for half in (() if kvonly else range(2)):
                    pg = nps()
                    for k in range(8):
                        mm(pg, pg.t[:, :], xnT.t[:, k, :], w_in1[:, k, 1280 + half * 512:1280 + (half + 1) * 512], [xnT, W],
                           start=(k == 0), stop=(k == 7), signal=(k == 7))
                    act(tg1.t[:], pg.t[:], AF.Tanh, [pg], [tg1], scale=0.5); act(tg2.t[:], pg.t[:], AF.Identity, [pg], [tg2], scale=0.5)
                    stt(sg1.t[:, half * 512:(half + 1) * 512], tg1.t[:], 1.0, tg2.t[:], ADD, MUL, [tg1, tg2], [sg1])
                    yield 3.3
                yield 'A'
                act(sq1.t[:], qk.t[:], AF.Square, [qk], [sq1])
                yield 0
                V('dve', lambda e: e.tensor_reduce(out=ss18.t[:], in_=sq1.t[:], axis=AX.X, op=ADD), [sq1], [ss18])
                yield 0
                act(ss18.t[:], ss18.t[:], AF.Ln, [ss18], [ss18], scale=1.0 / 64, bias=EPS)
                act(ss18.t[:], ss18.t[:], AF.Exp, [ss18], [ss18], scale=-0.5)
                yield 0
                tt(qk.t[:], qk.t[:], ss18.t[:].unsqueeze(2).to_broadcast([128, 18, 64]), MUL, [qk, ss18], [qk])
                tt(qk.t[:, 0:16, :], qk.t[:, 0:16, :], gq_b.t[:].unsqueeze(1).to_broadcast([128, 16, 64]), MUL, [qk, gq_b], [qk])
                tt(qk.t[:, 16:18, :], qk.t[:, 16:18, :], gk_b.t[:].unsqueeze(1).to_broadcast([128, 2, 64]), MUL, [qk, gk_b], [qk])
                cosb = rp.t[:, 0:32].unsqueeze(1).to_broadcast([128, 18, 32]); sinb = rp.t[:, 32:64].unsqueeze(1).to_broadcast([128, 18, 32])
                x1 = qk.t[:, :, 0:32]; x2 = qk.t[:, :, 32:64]
                tt(ra.t[:], x1, cosb, MUL, [qk, rp], [ra]); tt(rb.t[:], x2, sinb, MUL, [qk, rp], [rb])
                tt(qr.t[:, :, 0:32], ra.t[:], rb.t[:], SUB, [ra, rb], [qr])
                tt(ra.t[:], x2, cosb, MUL, [qk, rp], [ra]); tt(rb.t[:], x1, sinb, MUL, [qk, rp], [rb])
                tt(qr.t[:, :, 32:64], ra.t[:], rb.t[:], ADD, [ra, rb], [qr])
                yield 0
                cp(kfp.t[:], qr.t[:, 16:18, :].rearrange("p h d -> p (h d)"), [qr], [kfp], E='act')
                cp(kb16.t[:], kfp.t[:], [kfp], [kb16], E='act')
                p = nps(); pb = p.t[:].bitcast(BF16)
                tr(p, pb[:, 0:128], kb16.t[:], idb.t[:], [kb16, idb])
                cp(kT_c.t[:], pb[:, 0:128], [p], [kT_c])
                if kvonly:
                    return
                cp(qp.t[:, 0:8, 0:64], qr.t[:, 0:8, :], [qr], [qp]); cp(qp.t[:, 8:16, 64:128], qr.t[:, 8:16, :], [qr], [qp])
                yield 0
                for half in range(2):
                    p = nps(); pb = p.t[:].bitcast(BF16)
                    for j in range(8):
                        tr(p, pb[:, j * 128:(j + 1) * 128], qp.t[:, 8 * half + j, :], idb.t[:], [qp, idb], signal=(j == 7))
                    cp(qpT.t[:, 8 * half:8 * half + 8, :].rearrange("p h t -> p (h t)"), pb[:, 0:1024], [p], [qpT], E=('act' if half else 'dve'))
                yield 0
                blocks = ([(pmask, prev[0], prev[1])] if prev is not None else []) + [("cur", kT_c, va_c)]
                if packed:
                    NQ = 64
                    for hg in range(4):
                        kvh = hg // 2
                        pacc = PACC[hg % 2]
                        qv4 = qpT.t[:, 4 * hg:4 * hg + 4, 0:NQ]
                        for bi in range(NSMP + 1):
                            if bi < NSMP:
                                kT_b = _V(ckT_all, ckT_all.t[:, bi, :]); va_ap = cva_all.t[:, bi, kvh, :]; va_tb = cva_all
                                mk = Bm.t[:, 60 - 4 * bi:60 - 4 * bi + NQ].unsqueeze(1).to_broadcast([128, 4, NQ]); mk_tb = Bm
                            else:
                                kT_b = kT_c; va_ap = va_c.t[:, kvh, :]; va_tb = va_c
                                mk = msn.t[:].unsqueeze(1).to_broadcast([128, 4, NQ]); mk_tb = msn
                            p = nps()
                            pv3 = p.t[:, 0:4 * NQ].rearrange("p (h q) -> p h q", h=4)
                            mm(p, pv3, kT_b.t[:], qv4, [kT_b, qpT], start=True, stop=False, signal=False)
                            mm(p, pv3, idb.t[:], mk, [idb, mk_tb], start=False, stop=True)
                            px = pex[bi % 2]
                            act(px.t[:, 0:4 * NQ], p.t[:, 0:4 * NQ], AF.Exp, [p], [px], scale=0.125)
                            for j in range(4):
                                mm(pacc, pacc.t[0:NQ, j * 68:(j + 1) * 68], px.t[:, j * NQ:(j + 1) * NQ], va_ap, [px, va_tb],
                                   start=(bi == 0 and j == 0), stop=(bi == NSMP), signal=(j == 3))
                            yield 1.2
                        pav = pacc.t[:, 0:272].rearrange("p (h e) -> p h e", h=4)
                        tt(dn.t[:], pav[:, :, 64], esk.t[:, 4 * hg:4 * hg + 4], ADD, [pacc, esk], [dn])
                        V('dve', lambda e: e.reciprocal(out=dn.t[:], in_=dn.t[:]), [dn], [dn])
                        tt(o4.t[:], pav[:, :, 0:64], dn.t[:].unsqueeze(2).to_broadcast([128, 4, 64]), MUL, [pacc, dn], [o4])
                        tt(og.t[:, hg * 256:(hg + 1) * 256], o4.t[:].rearrange("p h d -> p (h d)"), sg1.t[:, hg * 256:(hg + 1) * 256], MUL, [o4, sg1], [og])
                        yield 1.0
                for hg in (() if packed else range(4)):
                    kvh = hg // 2
                    for bi, (nm, kT_b, va_b) in enumerate(blocks):
                        p = nps()
                        mm(p, p.t[:, :], kT_b.t[:], qpT.t[:, 4 * hg:4 * hg + 4, :].rearrange("p h t -> p (h t)"), [kT_b, qpT],
                           start=True, stop=False, signal=False)
                        mm(p, p.t[:, :], idb.t[:], mk4[nm].t[:].rearrange("p h t -> p (h t)"), [idb, mk4[nm]], start=False, stop=True)
                        act(pex[bi].t[:], p.t[:], AF.Exp, [p], [pex[bi]], scale=0.125)
                    yield 2.0
                    pacc = PACC[hg % 2]
                    for j in range(4):
                        for bi, (nm, kT_b, va_b) in enumerate(blocks):
                            mm(pacc, pacc.t[:, j * 68:(j + 1) * 68], pex[bi].t[:, j * 128:(j + 1) * 128], va_b.t[:, kvh, :], [pex[bi], va_b],
                               start=(bi == 0), stop=(bi == len(blocks) - 1), signal=(bi == len(blocks) - 1 and j == 3))
                    yield 0
                    pav = pacc.t[:, 0:272].rearrange("p (h e) -> p h e", h=4)
                    tt(dn.t[:], pav[:, :, 64], esk.t[:, 4 * hg:4 * hg + 4], ADD, [pacc, esk], [dn])
                    V('dve', lambda e: e.reciprocal(out=dn.t[:], in_=dn.t[:]), [dn], [dn])
                    tt(o4.t[:], pav[:, :, 0:64], dn.t[:].unsqueeze(2).to_broadcast([128, 4, 64]), MUL, [pacc, dn], [o4])
                    tt(og.t[:, hg * 256:(hg + 1) * 256], o4.t[:].rearrange("p h d -> p (h d)"), sg1.t[:, hg * 256:(hg + 1) * 256], MUL, [o4, sg1], [og])
                    yield 0
                p = nps(); pb = p.t[:].bitcast(BF16)
                for k in range(8):
                    tr(p, pb[:, k * 128:(k + 1) * 128], og.t[:, k * 128:(k + 1) * 128], idb.t[:], [og, idb], signal=(k == 7))
                yield 0
                cp(ogT.t[:].rearrange("p k t -> p (k t)"), pb[:, 0:1024], [p], [ogT], E='act')
                yield 0
                for half in range(2):
                    p = nps()
                    for k in range(8):
                        mm(p, p.t[:, :], ogT.t[:, k, :], w_out1[:, k, half * 512:(half + 1) * 512], [ogT, W],
                           start=(k == 0), stop=(k == 7), signal=(k == 7))
                    tt(yt.t[:, half * 512:(half + 1) * 512], p.t[:], xt.t[:, half * 512:(half + 1) * 512], ADD, [p, xt], [yt])
                T.dma('pool', bs(yt), ydst, yt.t[0:yrows, :], reads=[yt.b])
                if packed:
                    for s in range(NSMP):
                        T.dma('sp', s_out, o_k_s[s, 124:128, :], kfp.t[4 * s:4 * s + 4, :], reads=[kfp.b], serial=False)
                        T.dma('sp', s_out, o_v_s[s, 124:128, :], vfp.t[4 * s:4 * s + 4, :], reads=[vfp.b], serial=False)
                elif ydst is y_last:
                    T.dma('sp', bs(kfp), o_k_p, kfp.t[:], reads=[kfp.b]); T.dma('sp', bs(vfp), o_v_p, vfp.t[:], reads=[vfp.b])
            return layer1_tile

        L1 = [make_l1(0), make_l1(1)]
        y_last = y_p[(NMAIN - 1) * 128:NMAIN * 128, :]
        jobs = []
        for t in range(NMAIN + 1):
            cur = (kTs[t % 3], vas[t % 3]); prevb = (kTs[(t - 1) % 3], vas[(t - 1) % 3])
            if t == 0:
                f = lambda slot, xtb, cur=cur: L1[slot](h1_p[0:128, :], Bh1p, rope_p[0:128, :], None, cur, None, 0, kvonly=True, xtb=xtb)
                f.xsrc = (h1_p[0:128, :], 128, [Bh1p]); jobs.append(f)
            else:
                ydst = y_last if t == NMAIN else y_p[(t - 1) * 128:t * 128, :]
                f = lambda slot, xtb, t=t, cur=cur, prevb=prevb, ydst=ydst: L1[slot](
                    h1_p[t * 128:(t + 1) * 128, :], Bh1p, rope_p[t * 128:(t + 1) * 128, :], prevb, cur, ydst, 128,
                    pmask=('prev0' if t == 1 else 'prev'), xtb=xtb)
                f.xsrc = (h1_p[t * 128:(t + 1) * 128, :], 128, [Bh1p]); jobs.append(f)
        if with_sample:
            f = lambda slot, xtb: L1[slot](h1_s, Bh1s, rope_s, None, None, y_s, 64, smp='packed', xtb=xtb)
            f.xsrc = (h1_s, 64, [Bh1s]); jobs.insert(0, f)
        run_jobs(jobs, XT_B)
        T.finish('sp')
    return nc


def _prep_common(inputs):
    f = lambda k: np.ascontiguousarray(np.asarray(inputs[k], dtype=np.float32))
    c = {}
    c["e_ng"] = f("even_norm_g").reshape(1, D); c["e_win"] = f("even_w_in")[0]
    c["lam_re"] = f("s5_lambda_re")[0]; c["lam_im"] = f("s5_lambda_im")[0]; c["log_dt"] = f("s5_log_dt").reshape(1, 32)
    c["b_re"] = f("s5_b_re")[0]; c["b_im"] = f("s5_b_im")[0]
    c["c_re"] = f("s5_c_re")[0].reshape(512, 64); c["c_im"] = f("s5_c_im")[0].reshape(512, 64)
    c["s5_d"] = f("s5_d").reshape(512, 1); c["w_glu"] = f("s5_w_glu")[0]; c["b_glu"] = f("s5_b_glu").reshape(512, 1)
    c["w_gate"] = f("gla_w_gate")[0]; c["b_gate"] = f("gla_b_gate").reshape(256, 1); c["gla_g"] = f("gla_norm_g").reshape(128, 1)
    c["e_wout"] = f("even_w_out")[0]
    c["o_ng"] = f("odd_norm_g").reshape(1, D); c["o_win"] = f("odd_w_in")[0]
    c["qg"] = f("swa_q_norm_g").reshape(1, 64); c["kg"] = f("swa_k_norm_g").reshape(1, 64)
    c["sinks"] = f("swa_sinks").reshape(1, 16); c["o_wout"] = f("odd_w_out")[0]
    return c


def _const_tables():
    c = {}
    half = 32
    inv = (10000.0 ** (-np.arange(half, dtype=np.float32) / half)).astype(np.float32)
    def rope(pos):
        ang = pos.astype(np.float32)[:, None] * inv[None, :]
        return np.concatenate([np.cos(ang), np.sin(ang)], axis=1).astype(np.float32)
    c["_rope"] = rope
    c["rope_s"] = rope(8192 + (np.arange(128) % 4))
    vm = np.zeros((1, 128), np.float32); vm[0, :4] = 1.0
    c["vmask_s"] = vm
    i = np.arange(128)
    same = (i[:, None] // 64) == (i[None, :] // 64)
    c["m_gla"] = (same & (i[:, None] <= i[None, :])).astype(np.float32)
    j = np.arange(64)
    c["m_gla_s"] = (((j[:, None] // 4) == (j[None, :] // 4)) & (j[:, None] <= j[None, :])).astype(np.float32)
    BIG = -240000.0
    bf = ml_dtypes.bfloat16
    c["m_prev"] = np.where(i[:, None] > i[None, :], 0.0, BIG).astype(bf)
    c["m_cur"] = np.where(i[:, None] <= i[None, :], 0.0, BIG).astype(bf)
    c["_m_none"] = np.full((128, 128), BIG, np.float32).astype(bf)
    cc = np.arange(128)[:, None]; xx = (np.arange(124) - 60)[None, :]
    c["m_sc"] = np.where((xx >= 0) & (xx < 4) & (cc > xx), 0.0, BIG).astype(bf)
    qq = np.arange(64)[None, :]
    c["m_sn"] = np.where((cc < 64) & ((cc // 4) == (qq // 4)) & (cc <= qq), 0.0, BIG).astype(bf)
    c["identf"] = np.eye(128, dtype=np.float32)
    r = np.ones((1, 128), np.float32); r[0, 0] = 0; r[0, 64] = 0
    c["rmask"] = r
    c["rmask_s"] = np.ones((1, 64), np.float32)
    c["krev"] = (127.0 - np.arange(128, dtype=np.float32)).reshape(1, 128)
    c["seqsel"] = np.zeros((64, NSMP), np.float32)
    return c


_CACHE = {}


def core_inputs(inputs, com, c, NPRE, NMAIN, seq_tiles, b=None, j=None):
    if b is None:
        b, j = c // 4, c % 4
    xp = np.asarray(inputs["x_prompt"], np.float32); xs = np.asarray(inputs["x_sample"], np.float32)
    m = {k: v for k, v in com.items() if not k.startswith("_")}
    first = j * NMAIN - 1
    npre_real = max(first, 0)
    xpre = np.zeros((max(NPRE, 1) * 128, D), np.float32)
    if npre_real:
        xpre[(NPRE - npre_real) * 128:NPRE * 128] = xp[b, 0:npre_real * 128]
    xmain = np.zeros(((NMAIN + 1) * 128, D), np.float32)
    lo = first * 128
    if first >= 0:
        xmain[:] = xp[b, lo:lo + (NMAIN + 1) * 128]
    else:
        xmain[128:] = xp[b, 0:NMAIN * 128]
    m["x_pre"] = xpre; m["x_main"] = xmain
    m["rope_p"] = com["_rope"](np.maximum(lo + np.arange((NMAIN + 1) * 128), 0))
    m["m_prev0"] = com["m_prev"] if first >= 0 else com["_m_none"]
    sl = slice(c * NSMP, (c + 1) * NSMP)
    xpad = np.zeros((NSMP, 128, D), np.float32); xpad[:, :4] = xs[sl]
    m["x_s"] = xpad.reshape(NSMP * 128, D)
    m["st_s5r"] = np.asarray(inputs["state_s5_re"], np.float32)[0, sl].reshape(NSMP, 2048)
    m["st_s5i"] = np.asarray(inputs["state_s5_im"], np.float32)[0, sl].reshape(NSMP, 2048)
    m["st_gla"] = np.asarray(inputs["state_gla"], np.float32)[0, sl]
    m["ck_in"] = np.asarray(inputs["cache_swa_k"], np.float32)[0, sl].reshape(NSMP, 128, 128)
    m["cv_in"] = np.asarray(inputs["cache_swa_v"], np.float32)[0, sl].reshape(NSMP, 128, 128)
    return {k: np.ascontiguousarray(v) for k, v in m.items()}


def kernel(**inputs):
    NMAIN = SEQ // 128 // 4
    NPRE = 3 * NMAIN - 1
    if "nc" not in _CACHE:
        _CACHE["nc"] = build_program(NPRE, NMAIN)
    nc = _CACHE["nc"]
    com = _prep_common(inputs)
    com.update(_const_tables())
    in_maps = [core_inputs(inputs, com, c, NPRE, NMAIN, SEQ // 128) for c in range(NCORES)]
    res = run_bass_kernel_spmd(nc, in_maps, core_ids=list(range(NCORES)))
    R = res.results
    y_p = np.stack([np.concatenate([R[4 * b + j]["y_p"] for j in range(4)]) for b in range(2)])
    y_s = np.concatenate([R[c]["y_s"].reshape(NSMP, 4, D) for c in range(NCORES)])
    last = [3, 7]
    def s5(name):
        return np.stack([R[c][name].reshape(2, 64, 16).transpose(2, 0, 1).reshape(32, 64) for c in last])[None]
    gla_p = np.stack([R[c]["o_gla_p"].reshape(4, 64, 128) for c in last])[None]
    k_p = np.stack([R[c]["o_k_p"].reshape(128, 2, 64) for c in last])[None]
    v_p = np.stack([R[c]["o_v_p"].reshape(128, 2, 64) for c in last])[None]
    cat = lambda name, shp: np.concatenate([R[c][name].reshape((NSMP,) + shp) for c in range(NCORES)])[None]
    return (y_p, y_s, s5("o_s5r_p"), s5("o_s5i_p"), gla_p, k_p, v_p,
            cat("o_s5r_s", (32, 64)), cat("o_s5i_s", (32, 64)), cat("o_gla_s", (4, 64, 128)),
            cat("o_k_s", (128, 2, 64)), cat("o_v_s", (128, 2, 64)))
```

```python
import math
from contextlib import ExitStack
import numpy as np
import ml_dtypes
import concourse.bass as bass
import concourse.mybir as mybir
from concourse.bass_utils import run_bass_kernel_spmd

F32 = mybir.dt.float32
BF16 = mybir.dt.bfloat16
AF = mybir.ActivationFunctionType
ALU = mybir.AluOpType
AX = mybir.AxisListType
EPS = 1e-6
NCORES = 8
SEQ = 8192
NSMP = 16
D = 1024
EIN = 2576
OIN = 2304


class Buf:
    def __init__(self, name):
        self.name = name
        self.writers = []
        self.readers = []
        self.psum = False


class TB:
    def __init__(self, t, name):
        self.t = t
        self.b = Buf(name)


class Tracker:
    def __init__(self, nc):
        self.nc = nc
        self.eng = {'pe': nc.tensor, 'act': nc.scalar, 'dve': nc.vector, 'pool': nc.gpsimd, 'sp': nc.sync}
        self.sem = {}
        self.cnt = {}
        for k in self.eng:
            self.sem[k] = nc.alloc_semaphore(name='s_' + k)
            self.cnt[k] = 0
        self.seen = {k: {} for k in self.eng}
        self.dma_sems = []

    def _wait(self, E, ev):
        kind, key, val = ev
        if kind == 'eng' and key == E and E in ('pe', 'sp'):
            return
        sem = self.sem[key] if kind == 'eng' else key
        sk = (kind, key if kind == 'eng' else id(key))
        if self.seen[E].get(sk, 0) >= val:
            return
        self.seen[E][sk] = val
        self.eng[E].wait_ge(sem, val)

    def _deps(self, E, reads, writes):
        evs = []
        for b in reads:
            evs += b.writers
            if b.psum:
                evs += [ev for ev in b.readers if not (ev[0] == 'eng' and ev[1] == E)]
        for b in writes:
            evs += b.writers + b.readers
        best = {}
        for ev in evs:
            k = (ev[0], ev[1] if ev[0] == 'eng' else id(ev[1]))
            if k not in best or best[k][2] < ev[2]:
                best[k] = ev
        for ev in best.values():
            self._wait(E, ev)

    def _record(self, ev, reads, writes):
        for b in reads:
            b.readers.append(ev)
            if len(b.readers) > 64:
                b.readers = b.readers[-64:] if False else b.readers
        for b in writes:
            b.writers = [ev]
            b.readers = []

    def op(self, E, fn, reads=(), writes=(), signal=True):
        reads = [x.b if hasattr(x, 'b') else x for x in reads]
        writes = [x.b if hasattr(x, 'b') else x for x in writes]
        self._deps(E, reads, writes)
        ins = fn(self.eng[E])
        if signal:
            self.cnt[E] += 1
            ins.then_inc(self.sem[E], 1)
            ev = ('eng', E, self.cnt[E])
        else:
            ev = ('eng', E, self.cnt[E] + 1)
        self._record(ev, reads, writes)
        return ins

    def new_dma_sem(self, name):
        h = [self.nc.alloc_semaphore(name=name), 0]
        self.dma_sems.append(h)
        return h

    def dma(self, Q, semh, out, in_, reads=(), writes=(), serial=True, **kw):
        reads = [x.b if hasattr(x, 'b') else x for x in reads]
        writes = [x.b if hasattr(x, 'b') else x for x in writes]
        self._deps(Q, reads, writes)
        if serial and semh[1] > 0:
            self._wait(Q, ('dma', semh[0], semh[1]))
        semh[1] += 16
        self.eng[Q].dma_start(out=out, in_=in_, **kw).then_inc(semh[0], 16)
        ev = ('dma', semh[0], semh[1])
        self._record(ev, reads, writes)

    def barrier(self):
        for E in self.eng:
            for F in self.eng:
                if F != E and self.cnt[F] > 0:
                    self._wait(E, ('eng', F, self.cnt[F]))
            for s, c in self.dma_sems:
                if c:
                    self._wait(E, ('dma', s, c))

    def finish(self, E='sp'):
        for s, c in self.dma_sems:
            if c:
                self.eng[E].wait_ge(s, c)


def build_program(NPRE=47, NMAIN=16, with_sample=True):
    nc = bass.Bass("TRN2", target_bir_lowering=False)
    TOK = NMAIN * 128

    def din(name, shape, dt=F32):
        return nc.dram_tensor(name, list(shape), dt, kind="ExternalInput").ap()

    def dout(name, shape, dt=F32):
        return nc.dram_tensor(name, list(shape), dt, kind="ExternalOutput").ap()

    x_pre = din("x_pre", [max(NPRE, 1) * 128, D])
    x_main = din("x_main", [TOK + 128, D])
    x_s = din("x_s", [NSMP * 128, D])
    st_s5r = din("st_s5r", [NSMP, 2048])
    st_s5i = din("st_s5i", [NSMP, 2048])
    st_gla = din("st_gla", [NSMP, 4, 64, 128])
    ck_in = din("ck_in", [NSMP, 128, 128])
    cv_in = din("cv_in", [NSMP, 128, 128])
    e_ng = din("e_ng", [1, D]); e_win = din("e_win", [D, EIN])
    lam_re = din("lam_re", [32, 64]); lam_im = din("lam_im", [32, 64]); log_dt = din("log_dt", [1, 32])
    b_re = din("b_re", [32, 64, 16]); b_im = din("b_im", [32, 64, 16])
    c_re = din("c_re", [512, 64]); c_im = din("c_im", [512, 64])
    s5_d = din("s5_d", [512, 1]); w_glu = din("w_glu", [512, 512]); b_glu = din("b_glu", [512, 1])
    w_gate = din("w_gate", [16, 256]); b_gate = din("b_gate", [256, 1]); gla_g = din("gla_g", [128, 1])
    e_wout = din("e_wout", [D, D])
    o_ng = din("o_ng", [1, D]); o_win = din("o_win", [D, OIN]); qg = din("qg", [1, 64]); kg = din("kg", [1, 64])
    sinks = din("sinks", [1, 16]); o_wout = din("o_wout", [D, D])
    rope_p = din("rope_p", [TOK + 128, 64])
    rope_s = din("rope_s", [128, 64])
    vmask_s = din("vmask_s", [1, 128])
    m_gla = din("m_gla", [128, 128])
    m_gla_s = din("m_gla_s", [64, 64])
    m_prev = din("m_prev", [128, 128], BF16)
    m_cur = din("m_cur", [128, 128], BF16)
    m_prev0 = din("m_prev0", [128, 128], BF16)
    m_sc = din("m_sc", [128, 124], BF16)
    m_sn = din("m_sn", [128, 64], BF16)
    identf = din("identf", [128, 128])
    rmask = din("rmask", [1, 128])
    rmask_s = din("rmask_s", [1, 64])
    krev = din("krev", [1, 128])
    seqsel = din("seqsel", [64, NSMP])

    y_p = dout("y_p", [TOK, D]); y_s = dout("y_s", [NSMP * 4, D])
    o_s5r_p = dout("o_s5r_p", [128, 16]); o_s5i_p = dout("o_s5i_p", [128, 16])
    o_gla_p = dout("o_gla_p", [2, 128, 128])
    o_k_p = dout("o_k_p", [128, 128]); o_v_p = dout("o_v_p", [128, 128])
    o_s5r_s = dout("o_s5r_s", [NSMP, 2048]); o_s5i_s = dout("o_s5i_s", [NSMP, 2048])
    o_gla_s = dout("o_gla_s", [NSMP, 4, 64, 128])
    o_k_s = dout("o_k_s", [NSMP, 128, 128]); o_v_s = dout("o_v_s", [NSMP, 128, 128])
    h1_p = nc.dram_tensor("h1_p", [TOK + 128, D], F32, kind="Internal").ap()
    h1_s = nc.dram_tensor("h1_s", [128, D], F32, kind="Internal").ap()
    Bh1p = Buf("h1p"); Bh1s = Buf("h1s"); Bckout = Buf("ckout")

    es = ExitStack()
    with es:
        T = Tracker(nc)
        cnt = [0]

        cur_es = [es]

        def sb(shape, dt=F32, name=None):
            cnt[0] += 1
            nm = (name or "t") + str(cnt[0])
            return TB(cur_es[0].enter_context(nc.sbuf_tensor(nm, list(shape), dt)), nm)

        PS = []
        for i in range(8):
            PS.append(TB(es.enter_context(nc.psum_tensor("ps%d" % i, [128, 512], F32)), "ps%d" % i))
            PS[-1].b.psum = True
        psrr = [0]

        def nps():
            p = PS[psrr[0] % 6]
            psrr[0] += 1
            return p

        cpool = [T.new_dma_sem("cst%d" % i) for i in range(8)]
        cidx = [0]

        def csem():
            cidx[0] += 1
            return cpool[cidx[0] % 8]

        def bs(tb):
            if not hasattr(tb, 'sem'):
                tb.sem = T.new_dma_sem("b_" + tb.b.name)
            return tb.sem
        s_w = T.new_dma_sem("w")
        s_x = T.new_dma_sem("x")
        s_out = T.new_dma_sem("out")
        s_h = T.new_dma_sem("h")
        s_misc = T.new_dma_sem("misc")

        def ld(dst, ap_out, ap_in, q='sp', sem=None):
            T.dma(q, sem or csem(), ap_out, ap_in, writes=[dst])

        def V(E, fn, rd, wr):
            return T.op(E, fn, reads=rd, writes=wr)

        def mm(ps, out, lhsT, rhs, rd, start=True, stop=True, signal=True):
            return T.op('pe', lambda e: e.matmul(out, lhsT=lhsT, rhs=rhs, start=start, stop=stop),
                        reads=rd, writes=[ps], signal=signal)

        def tr(ps, out, in_, ident, rd, signal=True):
            return T.op('pe', lambda e: e.transpose(out, in_, ident), reads=rd, writes=[ps], signal=signal)

        idf = sb([128, 128]); ld(idf, idf.t[:], identf)
        idb = sb([128, 128], BF16); V('dve', lambda e: e.tensor_copy(out=idb.t[:], in_=idf.t[:]), [idf], [idb])
        ones_b = sb([128, 128], BF16); V('dve', lambda e: e.memset(ones_b.t[:], 1.0), [], [ones_b])
        W = sb([128, 8 * EIN + 8 * D + 4 * 512], BF16, "W")
        w_in0 = W.t[:, 0:8 * EIN].rearrange("p (k c) -> p k c", k=8)
        w_out0 = W.t[:, 8 * EIN:8 * EIN + 8 * D].rearrange("p (k c) -> p k c", k=8)
        w_gl = W.t[:, 8 * EIN + 8 * D:].rearrange("p (k c) -> p k c", k=4)
        W2b = Buf("W2"); s_w2 = T.new_dma_sem("w2")
        for k in range(8):
            T.dma('pool', s_w, w_in0[:, k, :], e_win[k * 128:(k + 1) * 128, :], writes=[W.b], serial=False)
        for k in range(8):
            T.dma('pool', s_w2, w_out0[:, k, :], e_wout[k * 128:(k + 1) * 128, :], writes=[W2b], serial=False)
        for k in range(4):
            T.dma('pool', s_w2, w_gl[:, k, :], w_glu[k * 128:(k + 1) * 128, :], writes=[W2b], serial=False)
        g0 = sb([128, D]); ld(g0, g0.t[:], e_ng.partition_broadcast(128))
        dcol = sb([128, 4]); bgl = sb([128, 4])
        for c in range(4):
            ld(dcol, dcol.t[:, c:c + 1], s5_d[c * 128:(c + 1) * 128, :])
            ld(bgl, bgl.t[:, c:c + 1], b_glu[c * 128:(c + 1) * 128, :])
        bglh = sb([128, 4])
        V('dve', lambda e: e.tensor_scalar(out=bglh.t[:], in0=bgl.t[:], scalar1=0.5, scalar2=None, op0=ALU.mult), [bgl], [bglh])
        wg = sb([16, 256]); ld(wg, wg.t[:], w_gate)
        nbg = sb([128, 2])
        for hp in range(2):
            ld(nbg, nbg.t[:, hp:hp + 1], b_gate[hp * 128:(hp + 1) * 128, :])
        V('dve', lambda e: e.tensor_scalar(out=nbg.t[:], in0=nbg.t[:], scalar1=-1.0, scalar2=None, op0=ALU.mult), [nbg], [nbg])
        gng = sb([128, 1]); ld(gng, gng.t[:], gla_g)
        mg = sb([128, 128]); ld(mg, mg.t[:], m_gla)
        rm2 = sb([128, 256])
        for hp in range(2):
            ld(rm2, rm2.t[:, hp * 128:(hp + 1) * 128], rmask.partition_broadcast(128))

        def tt(out, in0, in1, op, rd, wr, E='dve'):
            return V(E, lambda e: e.tensor_tensor(out=out, in0=in0, in1=in1, op=op), rd, wr)

        def ts(out, in0, s1, s2, op0, op1, rd, wr, E='dve'):
            if s2 is None:
                return V(E, lambda e: e.tensor_scalar(out=out, in0=in0, scalar1=s1, scalar2=None, op0=op0), rd, wr)
            return V(E, lambda e: e.tensor_scalar(out=out, in0=in0, scalar1=s1, scalar2=s2, op0=op0, op1=op1), rd, wr)

        def stt(out, in0, scalar, in1, op0, op1, rd, wr, E='dve'):
            return V(E, lambda e: e.scalar_tensor_tensor(out=out, in0=in0, scalar=scalar, in1=in1, op0=op0, op1=op1), rd, wr)

        def act(out, in_, func, rd, wr, **kw):
            return V('act', lambda e: e.activation(out=out, in_=in_, func=func, **kw), rd, wr)

        def cp(out, in_, rd, wr, E='dve'):
            if E == 'act':
                return V(E, lambda e: e.activation(out=out, in_=in_, func=AF.Copy), rd, wr)
            return V(E, lambda e: e.tensor_copy(out=out, in_=in_), rd, wr)

        MUL, ADD, SUB = ALU.mult, ALU.add, ALU.subtract
        PI = math.pi

        esA = ExitStack(); esA.__enter__(); cur_es[0] = esA
        cm = sb([128, 16, 128]); sm = sb([128, 16, 128]); rho = sb([128, 16])
        WT = sb([128, 16, 2, 128], BF16, "WT"); Bpr = sb([128, 16, 32]); Bpi = sb([128, 16, 32])
        L128r = sb([128, 16]); L128i = sb([128, 16])
        BbT = sb([128, 16, 2, 128], BF16, "BbT"); V('pool', lambda e: e.memset(BbT.t[:], 0.0), [], [BbT])
        CT = sb([128, 16, 2, 128], BF16, "CT"); V('pool', lambda e: e.memset(CT.t[:], 0.0), [], [CT])
        esS = ExitStack(); esS.__enter__(); cur_es[0] = esS
        lre = sb([128, 16]); lim = sb([128, 16]); ldt = sb([128, 16])
        for gpar in range(2):
            rows = slice(gpar * 64, (gpar + 1) * 64)
            T.dma('sp', csem(), lre.t[rows, :], lam_re.rearrange("(a b) p -> b p a", b=2)[gpar], writes=[lre.b],
                  allow_slow_non_contiguous=True)
            T.dma('sp', csem(), lim.t[rows, :], lam_im.rearrange("(a b) p -> b p a", b=2)[gpar], writes=[lim.b],
                  allow_slow_non_contiguous=True)
            T.dma('sp', csem(), ldt.t[rows, :],
                  log_dt.rearrange("o (a b) -> o b a", b=2)[:, gpar, :].partition_broadcast(64), writes=[ldt.b],
                  allow_slow_non_contiguous=True)
        dtt = sb([128, 16]); act(dtt.t[:], ldt.t[:], AF.Exp, [ldt], [dtt])
        aa = sb([128, 16]); tt(aa.t[:], lre.t[:], dtt.t[:], MUL, [lre, dtt], [aa])
        th = sb([128, 16]); tt(th.t[:], lim.t[:], dtt.t[:], MUL, [lim, dtt], [th])
        tmp16 = sb([128, 16])

        def wrap(t, n):
            for _ in range(n):
                ts(tmp16.t[:], t.t[:], PI, 2 * PI, ALU.is_gt, MUL, [t], [tmp16])
                tt(t.t[:], t.t[:], tmp16.t[:], SUB, [t, tmp16], [t])
        wrap(th, 5)
        th2 = sb([128, 16]); ts(th2.t[:], th.t[:], PI / 2, None, ADD, None, [th], [th2]); wrap(th2, 1)
        sn = sb([128, 16]); act(sn.t[:], th.t[:], AF.Sin, [th], [sn])
        cs = sb([128, 16]); act(cs.t[:], th2.t[:], AF.Sin, [th2], [cs])
        act(rho.t[:], aa.t[:], AF.Exp, [aa], [rho])
        lbr = sb([128, 16]); tt(lbr.t[:], rho.t[:], cs.t[:], MUL, [rho, cs], [lbr])
        lbi = sb([128, 16]); tt(lbi.t[:], rho.t[:], sn.t[:], MUL, [rho, sn], [lbi])
        nr = sb([128, 16]); ts(nr.t[:], lbr.t[:], -1.0, None, ADD, None, [lbr], [nr])
        den = sb([128, 16]); t2 = sb([128, 16])
        tt(den.t[:], lre.t[:], lre.t[:], MUL, [lre], [den]); tt(t2.t[:], lim.t[:], lim.t[:], MUL, [lim], [t2])
        tt(den.t[:], den.t[:], t2.t[:], ADD, [den, t2], [den])
        V('dve', lambda e: e.reciprocal(out=den.t[:], in_=den.t[:]), [den], [den])
        wr_ = sb([128, 16]); wi_ = sb([128, 16])
        tt(wr_.t[:], nr.t[:], lre.t[:], MUL, [nr, lre], [wr_]); tt(t2.t[:], lbi.t[:], lim.t[:], MUL, [lbi, lim], [t2])
        tt(wr_.t[:], wr_.t[:], t2.t[:], ADD, [wr_, t2], [wr_]); tt(wr_.t[:], wr_.t[:], den.t[:], MUL, [wr_, den], [wr_])
        tt(wi_.t[:], lbi.t[:], lre.t[:], MUL, [lbi, lre], [wi_]); tt(t2.t[:], nr.t[:], lim.t[:], MUL, [nr, lim], [t2])
        tt(wi_.t[:], wi_.t[:], t2.t[:], SUB, [wi_, t2], [wi_]); tt(wi_.t[:], wi_.t[:], den.t[:], MUL, [wi_, den], [wi_])
        cp(cm.t[:, :, 0:1], cs.t[:].unsqueeze(2), [cs], [cm]); cp(sm.t[:, :, 0:1], sn.t[:].unsqueeze(2), [sn], [sm])
        ta = sb([128, 16, 64]); tb = sb([128, 16, 64])
        n = 1
        while n < 128:
            br_ = cm.t[:, :, n - 1:n].to_broadcast([128, 16, n]); bi_ = sm.t[:, :, n - 1:n].to_broadcast([128, 16, n])
            a_r = cm.t[:, :, 0:n]; a_i = sm.t[:, :, 0:n]
            tt(ta.t[:, :, 0:n], a_r, br_, MUL, [cm, sm], [ta]); tt(tb.t[:, :, 0:n], a_i, bi_, MUL, [cm, sm], [tb])
            tt(cm.t[:, :, n:2 * n], ta.t[:, :, 0:n], tb.t[:, :, 0:n], SUB, [ta, tb], [cm])
            tt(ta.t[:, :, 0:n], a_r, bi_, MUL, [cm, sm], [ta]); tt(tb.t[:, :, 0:n], a_i, br_, MUL, [cm, sm], [tb])
            tt(sm.t[:, :, n:2 * n], ta.t[:, :, 0:n], tb.t[:, :, 0:n], ADD, [ta, tb], [sm])
            n *= 2
        blr = sb([128, 16, 16]); bli = sb([128, 16, 16])
        for gpar in range(2):
            rows = slice(gpar * 64, (gpar + 1) * 64)
            ld(blr, blr.t[rows], b_re.rearrange("(a b) p c -> b p a c", b=2)[gpar])
            ld(bli, bli.t[rows], b_im.rearrange("(a b) p c -> b p a c", b=2)[gpar])
        bbr = sb([128, 16, 16]); bbi = sb([128, 16, 16]); t3 = sb([128, 16, 16])
        wrb = wr_.t[:].unsqueeze(2).to_broadcast([128, 16, 16]); wib = wi_.t[:].unsqueeze(2).to_broadcast([128, 16, 16])
        tt(bbr.t[:], blr.t[:], wrb, MUL, [blr, wr_], [bbr]); tt(t3.t[:], bli.t[:], wib, MUL, [bli, wi_], [t3])
        tt(bbr.t[:], bbr.t[:], t3.t[:], SUB, [bbr, t3], [bbr])
        tt(bbi.t[:], bli.t[:], wrb, MUL, [bli, wr_], [bbi]); tt(t3.t[:], blr.t[:], wib, MUL, [blr, wi_], [t3])
        tt(bbi.t[:], bbi.t[:], t3.t[:], ADD, [bbi, t3], [bbi])
        stg = sb([16, 16, 2, 128], BF16)
        s_bbt = T.new_dma_sem("bbt")
        for gpair in range(16):
            for ri, src in ((0, bbr), (1, bbi)):
                p = nps()
                tr(p, p.t[0:16, 0:128], src.t[:, gpair, :], idf.t[:], [src, idf])
                cp(stg.t[:, gpair, ri, :], p.t[0:16, 0:128], [p], [stg])
        for g in range(32):
            gpair, gpar, gpos = g // 2, g % 2, g % 8
            for ri in range(2):
                T.dma('sp', s_bbt, BbT.t[gpos * 16:(gpos + 1) * 16, gpair, ri, gpar * 64:(gpar + 1) * 64],
                      stg.t[:, gpair, ri, gpar * 64:(gpar + 1) * 64], reads=[stg.b], writes=[BbT.b], serial=False)
        cls = [[sb([128, 128]) for _ in range(2)] for _ in range(4)]
        for c4 in range(4):
            for ri, src in ((0, c_re), (1, c_im)):
                cl = cls[c4][ri]
                ld(cl, cl.t[:, 0:64], src[c4 * 128:(c4 + 1) * 128, :]); ld(cl, cl.t[:, 64:128], src[c4 * 128:(c4 + 1) * 128, :])
        for c4 in range(4):
            for ri, src in ((0, c_re), (1, c_im)):
                cl = cls[c4][ri]
                p = nps()
                tr(p, p.t[:, 0:128], cl.t[:], idf.t[:], [cl, idf])
                for gpos in range(8):
                    g = 8 * c4 + gpos; gpar = g % 2
                    rows = slice(gpar * 64, (gpar + 1) * 64); cols = slice(gpos * 16, (gpos + 1) * 16)
                    ts(CT.t[rows, g // 2, ri, cols], p.t[rows, cols], 1.0 if ri == 0 else -1.0, None, MUL, None, [p], [CT])

        kr = sb([128, 128]); ld(kr, kr.t[:], krev.partition_broadcast(128))
        rrev = sb([128, 16, 128]); Wr_ = sb([128, 16, 128]); Wi_ = sb([128, 16, 128]); tq = sb([128, 16, 128])
        for gp_ in range(16):
            act(rrev.t[:, gp_, :], kr.t[:], AF.Exp, [kr, aa], [rrev], scale=aa.t[:, gp_:gp_ + 1])
        e128r = cm.t[:, :, 127:128].to_broadcast([128, 16, 128]); e128i = sm.t[:, :, 127:128].to_broadcast([128, 16, 128])
        tt(Wr_.t[:], cm.t[:], e128r, MUL, [cm], [Wr_]); tt(tq.t[:], sm.t[:], e128i, MUL, [sm], [tq])
        tt(Wr_.t[:], Wr_.t[:], tq.t[:], ADD, [Wr_, tq], [Wr_]); tt(Wr_.t[:], Wr_.t[:], rrev.t[:], MUL, [Wr_, rrev], [Wr_])
        tt(Wi_.t[:], cm.t[:], e128i, MUL, [cm, sm], [Wi_]); tt(tq.t[:], sm.t[:], e128r, MUL, [sm, cm], [tq])
        tt(Wi_.t[:], Wi_.t[:], tq.t[:], SUB, [Wi_, tq], [Wi_]); tt(Wi_.t[:], Wi_.t[:], rrev.t[:], MUL, [Wi_, rrev], [Wi_])
        for gp_ in range(16):
            for ri, src in ((0, Wr_), (1, Wi_)):
                p = nps()
                tr(p, p.t[:, 0:128], src.t[:, gp_, :], idf.t[:], [src, idf])
                cp(WT.t[:, gp_, ri, :], p.t[:, 0:128], [p], [WT], E=('act' if ri else 'dve'))
        V('pool', lambda e: e.memset(Bpr.t[:], 0.0), [], [Bpr]); V('pool', lambda e: e.memset(Bpi.t[:], 0.0), [], [Bpi])
        for half in range(2):
            rows = slice(half * 64, (half + 1) * 64); cols = slice(half * 16, (half + 1) * 16)
            cp(Bpr.t[rows, :, cols], bbr.t[rows, :, :], [bbr], [Bpr]); cp(Bpi.t[rows, :, cols], bbi.t[rows, :, :], [bbi], [Bpi])
        r128 = sb([128, 16]); act(r128.t[:], aa.t[:], AF.Exp, [aa], [r128], scale=128.0)
        tt(L128r.t[:], r128.t[:], cm.t[:, :, 127], MUL, [r128, cm], [L128r]); tt(L128i.t[:], r128.t[:], sm.t[:, :, 127], MUL, [r128, sm], [L128i])
        T.barrier()
        esS.close(); cur_es[0] = esA
        xcr = sb([128, 16]); xci = sb([128, 16])
        Sp = [sb([128, 128], name="S") for _ in range(2)]
        vmask = sb([128, 128]); ld(vmask, vmask.t[:], vmask_s.partition_broadcast(128))
        chain_of = {}
        POOLX = 'dve'

        def run_jobs(jobs, XT):
            active = []
            nxt = 0
            ready = True
            xassign = {}
            xfree = [0, 1, 2]

            def issue_x(i):
                if i >= len(jobs) or i in xassign or not xfree:
                    return
                k = xfree.pop(0); xassign[i] = k
                ap, rows, rd = jobs[i].xsrc
                xb = XT[k]
                T.dma('sp', bs(xb), xb.t[0:rows, :], ap[0:rows, :], reads=rd, writes=[xb.b])
            while nxt < len(jobs) or active:
                free = [s_ for s_ in (0, 1) if all(e[1] != s_ for e in active)]
                if nxt < len(jobs) and ready and free:
                    slot = free[0]
                    vt0 = min([e[2] for e in active], default=0.0)
                    issue_x(nxt)
                    assert nxt in xassign
                    active.append([jobs[nxt](slot, XT[xassign[nxt]]), slot, vt0, nxt]); nxt += 1; ready = False
                    issue_x(nxt)
                ent = min(active, key=lambda e: e[2])
                try:
                    r = next(ent[0])
                    if r == 'A':
                        if ent is active[-1]:
                            ready = True
                        r = 2.4
                    ent[2] += float(r) if r else 1.0
                except StopIteration:
                    if ent is active[-1]:
                        ready = True
                    active.remove(ent)
                    xfree.append(xassign[ent[3]])
                    issue_x(nxt)

        def make_l0(slot):
            xn = sb([128, D], BF16); xnT = sb([128, 8, 128], BF16)
            ssq = sb([128, 1]); rstd = sb([128, 1]); junk = xn
            s_h = T.new_dma_sem("h%d" % slot)
            rr = [0]

            def nps():
                p = PS[4 * slot + 1 + rr[0] % 3]
                rr[0] += 1
                return p
            PFIX = PS[4 * slot]
            uF = sb([128, 4, 128], BF16); sgA = sb([128, 4, 128], BF16); sgB = sb([128, 4, 128], BF16)
            qkF = sb([128, 4, 128]); alF = sb([16, 128]); vT = sb([128, 512], BF16)
            mixT = sb([128, 8, 128], BF16)
            w1 = sb([128, 512]); w2 = sb([128, 512]); mre = sb([128, 512]); mim = sb([128, 512])
            Xrb = sb([128, 4, 128], BF16); Xib = sb([128, 4, 128], BF16)
            yy = w1; y2 = w2; zf = mre; zz = sb([128, 4, 128], BF16)
            class _V:
                def __init__(self, tb, ap):
                    self.t = ap; self.b = tb.b
            spt = _V(w1, w1.t[:, 0:256]); bc = _V(w1, w1.t[:, 256:512]); eq = _V(w2, w2.t[:, 0:256]); ek = _V(w2, w2.t[:, 256:512])
            dtl = _V(mre, mre.t[:, 0:256])
            ech = sb([128, 4]); s1 = sb([128, 16]); s2 = sb([128, 16]); s3 = sb([128, 16]); s4 = sb([128, 16]); uT = sb([128, 512], BF16)
            qdA = sb([128, 256], BF16); qdB = sb([128, 256], BF16); kdA = sb([128, 256], BF16); kdB = sb([128, 256], BF16)
            for z_ in (qdA, qdB, kdA, kdB):
                V('pool', lambda e, z_=z_: e.memset(z_.t[:], 0.0), [], [z_])
            ktl = sb([128, 256], BF16); ktA = sb([128, 256], BF16); ktB = sb([128, 256], BF16)
            V('pool', lambda e: e.memset(ktA.t[:], 0.0), [], [ktA]); V('pool', lambda e: e.memset(ktB.t[:], 0.0), [], [ktB])
            att = sb([128, 512], BF16)
            Sb0 = [sb([128, 128], BF16) for _ in range(2)]
            Sb1 = [sb([128, 128], BF16) for _ in range(2)]
            osq = sb([128, 512], BF16); orst = mim; ob = mre
            hh = (mim, w2)

            def rmsnorm_T(xt, gt):
                act(junk.t[:], xt.t[:], AF.Square, [xt], [junk, ssq], accum_out=ssq.t[:])
                act(rstd.t[:], ssq.t[:], AF.Ln, [ssq], [rstd], scale=1.0 / D, bias=EPS)
                act(rstd.t[:], rstd.t[:], AF.Exp, [rstd], [rstd], scale=-0.5)
                stt(xn.t[:], xt.t[:], rstd.t[:, 0:1], gt.t[:], MUL, MUL, [xt, rstd, gt], [xn])
                p = nps()
                pb = p.t[:].bitcast(BF16)
                for k in range(8):
                    tr(p, pb[:, k * 128:(k + 1) * 128], xn.t[:, k * 128:(k + 1) * 128], idb.t[:], [xn, idb], signal=(k == 7))
                cp(xnT.t[:].rearrange("p k t -> p (k t)"), pb[:, 0:1024], [p], [xnT], E='act')

            def projF(col0, ntile, p):
                for c in range(ntile):
                    for k in range(8):
                        mm(p, p.t[:, c * 128:(c + 1) * 128], w_in0[:, k, col0 + c * 128:col0 + (c + 1) * 128], xnT.t[:, k, :],
                           [W, xnT], start=(k == 0), stop=(k == 7), signal=(k == 7 and c == ntile - 1))

            def layer0_tile(xsrc, h1dst, h1buf, endcol, smp, so=False, carry=None, pre=None, post=None, xtb=None):
                xt = xtb
                xcr, xci, Sp = carry[:3]
                nwp = 4 if smp else 128

                def F3(ap):
                    return ap.rearrange("p (c t) -> p c t", c=4)[:, :, 0:nwp]
                chain = chain_of.setdefault(id(xcr), {'n': 0, 'done': {}})
                my = chain['n']; chain['n'] += 1
                if pre is not None:
                    pre()
                rmsnorm_T(xt, g0)
                yield 3.0
                if so:
                    p = nps()
                    for k in range(8):
                        mm(p, p.t[:, :], xnT.t[:, k, :], w_in0[:, k, 0:512], [W, xnT], start=(k == 0), stop=(k == 7), signal=(k == 7))
                    cp(uT.t[:], p.t[:], [p], [uT], E='act')
                else:
                    p = nps(); projF(0, 4, p); cp(uF.t[:, :, 0:nwp], F3(p.t[:]), [p], [uF], E='act')
                yield (2.5 if so else 4.0)
                if not so:
                    p = nps(); projF(512, 4, p)
                    act(F3(w1.t[:]), F3(p.t[:]), AF.Tanh, [p], [w1], scale=0.5); act(F3(w2.t[:]), F3(p.t[:]), AF.Identity, [p], [w2], scale=0.125)
                    stt(sgA.t[:, :, 0:nwp], F3(w1.t[:]), 1.0, F3(w2.t[:]), ADD, MUL, [w1, w2], [sgA])
                    yield 4.5
                    p = nps(); projF(1024, 4, p); cp(qkF.t[:].rearrange("p c t -> p (c t)"), p.t[:], [p], [qkF], E='act')
                else:
                    p = nps(); projF(1280, 2, p); cp(qkF.t[:, 2:4, :].rearrange("p c t -> p (c t)"), p.t[:, 0:256], [p], [qkF], E='act')
                p = nps()
                for k in range(8):
                    mm(p, p.t[0:16, 0:128], w_in0[:, k, 2048:2064], xnT.t[:, k, :], [W, xnT], start=(k == 0), stop=(k == 7), signal=(k == 7))
                cp(alF.t[:], p.t[0:16, 0:128], [p], [alF], E='act')
                yield (3.2 if so else 5.2)
                if not so:
                    p = nps(); projF(2064, 4, p)
                    act(w1.t[:], p.t[:], AF.Tanh, [p], [w1], scale=0.5); act(w2.t[:], p.t[:], AF.Identity, [p], [w2], scale=0.5)
                    stt(sgB.t[:].rearrange("p c t -> p (c t)"), w1.t[:], 1.0, w2.t[:], ADD, MUL, [w1, w2], [sgB])
                    yield 4.5
                p = nps()
                for k in range(8):
                    mm(p, p.t[:, :], xnT.t[:, k, :], w_in0[:, k, 1536:2048], [W, xnT], start=(k == 0), stop=(k == 7), signal=(k == 7))
                cp(vT.t[:], p.t[:], [p], [vT], E='act')
                yield 'A'
                py = None if so else PFIX
                if so:
                    while chain['done'].get(('s5', 3), 0) < my:
                        yield 0
                    pR = nps(); pI = nps()
                    for gp_ in range(16):
                        for ri, pp in ((0, pR), (1, pI)):
                            mm(pp, pp.t[:, gp_ * 32:(gp_ + 1) * 32], WT.t[:, gp_, ri, :], uT.t[:, gp_ * 32:(gp_ + 1) * 32], [WT, uT],
                               signal=(gp_ == 15))
                    yield 3.4
                    bpr = Bpr.t[:].rearrange("p g c -> p (g c)"); bpi = Bpi.t[:].rearrange("p g c -> p (g c)")
                    w1g = w1.t[:].rearrange("p (g c) -> p g c", g=16)
                    tt(w1.t[:], pR.t[:], bpr, MUL, [pR, Bpr], [w1]); tt(w2.t[:], pI.t[:], bpi, MUL, [pI, Bpi], [w2])
                    tt(w1.t[:], w1.t[:], w2.t[:], SUB, [w1, w2], [w1])
                    V('dve', lambda e: e.tensor_reduce(out=s1.t[:], in_=w1g, axis=AX.X, op=ADD), [w1], [s1])
                    tt(w1.t[:], pR.t[:], bpi, MUL, [pR, Bpi], [w1]); tt(w2.t[:], pI.t[:], bpr, MUL, [pI, Bpr], [w2])
                    tt(w1.t[:], w1.t[:], w2.t[:], ADD, [w1, w2], [w1])
                    V('dve', lambda e: e.tensor_reduce(out=s2.t[:], in_=w1g, axis=AX.X, op=ADD), [w1], [s2])
                    tt(s3.t[:], L128r.t[:], xcr.t[:], MUL, [L128r, xcr], [s3]); tt(s4.t[:], L128i.t[:], xci.t[:], MUL, [L128i, xci], [s4])
                    tt(s3.t[:], s3.t[:], s4.t[:], SUB, [s3, s4], [s3]); tt(s1.t[:], s1.t[:], s3.t[:], ADD, [s1, s3], [s1])
                    tt(s3.t[:], L128r.t[:], xci.t[:], MUL, [L128r, xci], [s3]); tt(s4.t[:], L128i.t[:], xcr.t[:], MUL, [L128i, xcr], [s4])
                    tt(s3.t[:], s3.t[:], s4.t[:], ADD, [s3, s4], [s3]); tt(xci.t[:], s2.t[:], s3.t[:], ADD, [s2, s3], [xci])
                    cp(xcr.t[:], s1.t[:], [s1], [xcr])
                    for c4 in range(4):
                        chain['done'][('s5', c4)] = my + 1
                    yield 6.5
                nw = 4 if smp else 128

                def emit_BU(c4):
                    pr = nps(); pi = nps()
                    for gq in range(4):
                        for ri, pp in ((0, pr), (1, pi)):
                            mm(pp, pp.t[:, gq * 128:gq * 128 + nw], BbT.t[:, 4 * c4 + gq, ri, :], uF.t[:, c4, 0:nw], [BbT, uF],
                               signal=(gq == 3))
                    return pr, pi
                bu_next = None if so else emit_BU(0)
                for c4 in (() if so else range(4)):
                    while chain['done'].get(('s5', c4), 0) < my:
                        yield 0
                    pr, pi = bu_next
                    gs = slice(4 * c4, 4 * c4 + 4)
                    cmv = cm.t[:, gs, 0:nw]; smv = sm.t[:, gs, 0:nw]
                    prv = pr.t[:].rearrange("p (g t) -> p g t", g=4)[:, :, 0:nw]; piv = pi.t[:].rearrange("p (g t) -> p g t", g=4)[:, :, 0:nw]
                    w1v = w1.t[:].rearrange("p (g t) -> p g t", g=4)[:, :, 0:nw]; w2v = w2.t[:].rearrange("p (g t) -> p g t", g=4)[:, :, 0:nw]
                    mrv = mre.t[:].rearrange("p (g t) -> p g t", g=4)[:, :, 0:nw]; miv = mim.t[:].rearrange("p (g t) -> p g t", g=4)[:, :, 0:nw]
                    tt(w1v, prv, cmv, MUL, [pr, cm], [w1]); tt(w2v, piv, smv, MUL, [pi, sm], [w2])
                    tt(mrv, w1v, w2v, ADD, [w1, w2], [mre])
                    tt(w1v, piv, cmv, MUL, [pi, cm], [w1]); tt(w2v, prv, smv, MUL, [pr, sm], [w2])
                    tt(miv, w1v, w2v, SUB, [w1, w2], [mim])
                    yield (1.0 if smp else 4.2)
                    for gq in range(4):
                        gp_ = 4 * c4 + gq
                        V('dve', lambda e, gq=gq, gp_=gp_: e.tensor_tensor_scan(
                            out=mrv[:, gq, :], data0=rho.t[:, gp_:gp_ + 1].to_broadcast([128, nw]), data1=mrv[:, gq, :], initial=xcr.t[:, gp_:gp_ + 1],
                            op0=MUL, op1=ADD), [mre, rho, xcr], [mre])
                        V('dve', lambda e, gq=gq, gp_=gp_: e.tensor_tensor_scan(
                            out=miv[:, gq, :], data0=rho.t[:, gp_:gp_ + 1].to_broadcast([128, nw]), data1=miv[:, gq, :], initial=xci.t[:, gp_:gp_ + 1],
                            op0=MUL, op1=ADD), [mim, rho, xci], [mim])
                    yield (1.2 if smp else 3.0)
                    if so:
                        e_ = endcol
                        s1v = s1.t[:].unsqueeze(2); s2v = s2.t[:].unsqueeze(2)
                        tt(s1v, mrv[:, :, e_:e_ + 1], cmv[:, :, e_:e_ + 1], MUL, [mre, cm], [s1]); tt(s2v, miv[:, :, e_:e_ + 1], smv[:, :, e_:e_ + 1], MUL, [mim, sm], [s2])
                        tt(xcr.t[:, gs], s1.t[:], s2.t[:], SUB, [s1, s2], [xcr])
                        tt(s1v, miv[:, :, e_:e_ + 1], cmv[:, :, e_:e_ + 1], MUL, [mim, cm], [s1]); tt(s2v, mrv[:, :, e_:e_ + 1], smv[:, :, e_:e_ + 1], MUL, [mre, sm], [s2])
                        tt(xci.t[:, gs], s1.t[:], s2.t[:], ADD, [s1, s2], [xci])
                        chain['done'][('s5', c4)] = my + 1
                        yield 0
                        continue
                    tt(w1v, mrv, cmv, MUL, [mre, cm], [w1]); tt(w2v, miv, smv, MUL, [mim, sm], [w2], E=POOLX)
                    tt(Xrb.t[:, :, 0:nw], w1v, w2v, SUB, [w1, w2], [Xrb])
                    tt(xcr.t[:, gs], w1v[:, :, endcol], w2v[:, :, endcol], SUB, [w1, w2], [xcr])
                    tt(w1v, miv, cmv, MUL, [mim, cm], [w1]); tt(w2v, mrv, smv, MUL, [mre, sm], [w2], E=POOLX)
                    tt(Xib.t[:, :, 0:nw], w1v, w2v, ADD, [w1, w2], [Xib])
                    tt(xci.t[:, gs], w1v[:, :, endcol], w2v[:, :, endcol], ADD, [w1, w2], [xci])
                    yield (1.2 if smp else 4.5)
                    if c4 < 3:
                        bu_next = emit_BU(c4 + 1)
                    i = 0
                    for gq in range(4):
                        for ri, xb_ in ((0, Xrb), (1, Xib)):
                            mm(py, py.t[:, c4 * 128:c4 * 128 + nw], CT.t[:, 4 * c4 + gq, ri, :], xb_.t[:, gq, 0:nw], [CT, xb_],
                               start=(i == 0), stop=(i == 7), signal=(i == 7))
                            i += 1
                    chain['done'][('s5', c4)] = my + 1
                    yield 3.4
                if not so:
                  yv = F3(yy.t[:]); y2v = F3(y2.t[:]); zfv = F3(zf.t[:])
                  for c4 in range(4):
                      stt(yv[:, c4, :], uF.t[:, c4, 0:nwp], dcol.t[:, c4:c4 + 1], py.t[:, c4 * 128:c4 * 128 + nwp], MUL, ADD, [uF, dcol, py], [yy])
                  act(y2v, yv, AF.Square, [yy], [y2], scale=0.2114592159259085)
                  yield 0
                  stt(y2v, y2v, 1.0, yv, ADD, MUL, [y2, yy], [y2])
                  yield 0
                  act(y2v, y2v, AF.Tanh, [y2], [y2], scale=0.7978845608028654)
                  yield 0
                  stt(zfv, y2v, 1.0, yv, ADD, MUL, [y2, yy], [zf])
                  cp(zz.t[:, :, 0:nwp], zfv, [zf], [zz], E='act')
                  yield 0
                  p = nps()
                  for c in range(4):
                      for k in range(4):
                          mm(p, p.t[:, c * 128:c * 128 + nwp], w_gl[:, k, c * 128:(c + 1) * 128], zz.t[:, k, 0:nwp], [W2b, zz],
                             start=(k == 0), stop=(k == 3), signal=(k == 3 and c == 3))
                  for c in range(4):
                      act(y2v[:, c, :], p.t[:, c * 128:c * 128 + nwp], AF.Tanh, [p, bglh], [y2], scale=0.25, bias=bglh.t[:, c:c + 1])
                  yield 0
                  stt(zfv, y2v, 1.0, zfv, ADD, MUL, [y2, zf], [zf])
                  tt(mixT.t[:, 0:4, 0:nwp], zfv, sgA.t[:, :, 0:nwp], MUL, [zf, sgA], [mixT])
                if not so:
                    yield 0
                for hp in range(2):
                    while chain['done'].get(('gla', hp), 0) < my:
                        yield 0
                p = nps()
                for hp in range(2):
                    mm(p, p.t[:, hp * 128:(hp + 1) * 128], wg.t[:, hp * 128:(hp + 1) * 128], alF.t[:], [wg, alF])
                for hp in range(2):
                    act(spt.t[:, hp * 128:(hp + 1) * 128], p.t[:, hp * 128:(hp + 1) * 128], AF.Exp, [p, nbg], [spt], scale=-1.0, bias=nbg.t[:, hp:hp + 1])
                act(spt.t[:], spt.t[:], AF.Ln, [spt], [spt], bias=1.0)
                yield 1.5
                if smp:
                    tt(spt.t[:].rearrange("p (h t) -> p h t", h=2), spt.t[:].rearrange("p (h t) -> p h t", h=2),
                       vmask.t[:].unsqueeze(1).to_broadcast([128, 2, 128]), MUL, [spt, vmask], [spt])
                V('dve', lambda e: e.tensor_tensor_scan(out=bc.t[:], data0=rm2.t[:], data1=spt.t[:], initial=0.0, op0=MUL, op1=ADD),
                  [rm2, spt], [bc])
                yield 1.0
                bcv = bc.t[:].rearrange("p (c t) -> p c t", c=4)
                if not so:
                    act(eq.t[:], bc.t[:], AF.Exp, [bc], [eq], scale=-1.0 / 16)
                    act(ek.t[:], bc.t[:], AF.Exp, [bc], [ek], scale=1.0 / 16)
                tt(dtl.t[:].rearrange("p (c t) -> p c t", c=4), bcv[:, :, 63:64].to_broadcast([128, 4, 64]), bcv, SUB, [bc], [dtl])
                act(dtl.t[:], dtl.t[:], AF.Exp, [dtl], [dtl], scale=-1.0 / 16)
                act(ech.t[:], bcv[:, :, 63], AF.Exp, [bc], [ech], scale=-1.0 / 16)
                yield 1.5
                qv = qkF.t[:, 0:2, :].rearrange("p h t -> p (h t)"); kv = qkF.t[:, 2:4, :].rearrange("p h t -> p (h t)")
                if not so:
                    for half, (qd_, kd_) in enumerate(((qdA, kdA), (qdB, kdB))):
                        rows = slice(half * 64, (half + 1) * 64)
                        stt(qd_.t[rows, :], qv[rows, :], 0.125, eq.t[rows, :], MUL, MUL, [qkF, eq], [qd_])
                        tt(kd_.t[rows, :], kv[rows, :], ek.t[rows, :], MUL, [qkF, ek], [kd_])
                tt(ktl.t[:], kv, dtl.t[:], MUL, [qkF, dtl], [ktl])
                yield 1.5
                p = nps(); pb = p.t[:].bitcast(BF16)
                for hp in range(2):
                    tr(p, pb[:, hp * 128:(hp + 1) * 128], ktl.t[:, hp * 128:(hp + 1) * 128], idb.t[:], [ktl, idb], signal=(hp == 1))
                cp(ktA.t[0:64, :], pb[0:64, 0:256], [p], [ktA]); cp(ktB.t[64:128, :], pb[64:128, 0:256], [p], [ktB], E='act')
                yield 1.0
                if not so:
                    for hp in range(2):
                        cp(Sb0[hp].t[:], Sp[hp].t[:], [Sp[hp]], [Sb0[hp]], E='act')
                for c, kt_ in ((0, ktA), (1, ktB)):
                    pa = nps()
                    for h in range(4):
                        mm(pa, pa.t[:, h * 128:(h + 1) * 128], kt_.t[:, (h // 2) * 128:(h // 2 + 1) * 128], vT.t[:, h * 128:(h + 1) * 128],
                           [kt_, vT], signal=(h == 3))
                    for h in range(4):
                        hp, h2 = h // 2, h % 2
                        rows = slice(h2 * 64, (h2 + 1) * 64); S = Sp[hp]
                        stt(S.t[rows, :], S.t[rows, :], ech.t[rows, 2 * hp + c:2 * hp + c + 1], pa.t[rows, h * 128:(h + 1) * 128], MUL, ADD,
                            [S, ech, pa], [S])
                    if c == 0 and not so:
                        for hp in range(2):
                            cp(Sb1[hp].t[:], Sp[hp].t[:], [Sp[hp]], [Sb1[hp]], E='act')
                    yield 1.2
                for hp in range(2):
                    chain['done'][('gla', hp)] = my + 1
                if not so:
                    p = nps()
                    for h in range(4):
                        hp, h2 = h // 2, h % 2
                        kd_ = (kdA, kdB)[h2]; qd_ = (qdA, qdB)[h2]
                        mm(p, p.t[:, h * 128:(h + 1) * 128], kd_.t[:, hp * 128:(hp + 1) * 128], qd_.t[:, hp * 128:(hp + 1) * 128], [kd_, qd_],
                           signal=(h == 3))
                    yield 1.0
                    tt(att.t[:].rearrange("p (h t) -> p h t", h=4), p.t[:].rearrange("p (h t) -> p h t", h=4),
                       mg.t[:].unsqueeze(1).to_broadcast([128, 4, 128]), MUL, [p, mg], [att])
                    yield 1.0
                    po = nps()
                    for h in range(4):
                        hp, h2 = h // 2, h % 2
                        qd_ = (qdA, qdB)[h2]
                        c0 = h * 128
                        mm(po, po.t[:, c0:c0 + 128], vT.t[:, h * 128:(h + 1) * 128], att.t[:, c0:c0 + 128], [vT, att], start=True, stop=False, signal=False)
                        mm(po, po.t[:, c0:c0 + 64], Sb0[hp].t[:], qd_.t[:, hp * 128:hp * 128 + 64], [Sb0[hp], qd_], start=False, stop=False, signal=False)
                        mm(po, po.t[:, c0 + 64:c0 + 128], Sb1[hp].t[:], qd_.t[:, hp * 128 + 64:hp * 128 + 128], [Sb1[hp], qd_], start=False, stop=True,
                           signal=(h == 3))
                    yield 1.5
                    act(osq.t[:], po.t[:], AF.Square, [po], [osq])
                    yield 0.8
                    pn = nps()
                    mm(pn, pn.t[:, :], ones_b.t[:], osq.t[:], [ones_b, osq])
                    yield 0.6
                    act(orst.t[:], pn.t[:], AF.Ln, [pn], [orst], scale=1.0 / 128, bias=EPS)
                    act(orst.t[:], orst.t[:], AF.Exp, [orst], [orst], scale=-0.5)
                    yield 1.2
                    stt(ob.t[:], po.t[:], gng.t[:, 0:1], orst.t[:], MUL, MUL, [po, gng, orst], [ob])
                    tt(mixT.t[:, 4:8, :].rearrange("p h t -> p (h t)"), ob.t[:], sgB.t[:].rearrange("p h t -> p (h t)"), MUL, [ob, sgB], [mixT])
                    yield 1.5
                for half in (() if so else range(2)):
                    p = nps()
                    for k in range(8):
                        mm(p, p.t[:, :], mixT.t[:, k, :], w_out0[:, k, half * 512:(half + 1) * 512], [mixT, W2b],
                           start=(k == 0), stop=(k == 7), signal=(k == 7))
                    tt(hh[half].t[:], p.t[:], xt.t[:, half * 512:(half + 1) * 512], ADD, [p, xt], [hh[half]])
                    nr = 4 if smp else 128
                    T.dma('sp', s_h, h1dst[0:nr, half * 512:(half + 1) * 512], hh[half].t[0:nr, :], reads=[hh[half].b], writes=[h1buf])
                if post is not None:
                    post()
            return layer0_tile

        XT_A = [sb([128, D], name="xtA") for _ in range(3)]
        L0 = [make_l0(0), make_l0(1)]
        for z_ in (xcr, xci):
            V('dve', lambda e, z_=z_: e.memset(z_.t[:], 0.0), [], [z_])
        for hp in range(2):
            V('dve', lambda e, hp=hp: e.memset(Sp[hp].t[:], 0.0), [], [Sp[hp]])
        XS = [sb([128, 16, NSMP]) for _ in range(2)]; XO = XS
        stin = sb([NSMP, 512]); stout = stin
        scar = [(sb([128, 16]), sb([128, 16]), [sb([128, 128]), sb([128, 128])]) for _ in range(2)]
        srr = [0]
        scar_busy = [False, False]

        def sps():
            srr[0] += 1
            return PS[1 + srr[0] % 3]
        if with_sample:
            for ri, src in ((0, st_s5r), (1, st_s5i)):
                for q4 in range(4):
                    T.dma('sp', bs(stin), stin.t[:], src[:, q4 * 512:(q4 + 1) * 512], writes=[stin.b])
                    for gg in range(4):
                        gpair = 4 * q4 + gg
                        p = sps()
                        tr(p, p.t[:, 0:NSMP], stin.t[:, gg * 128:(gg + 1) * 128], idf.t[0:NSMP, 0:NSMP], [stin, idf])
                        cp(XS[ri].t[:, gpair, :], p.t[:, 0:NSMP], [p], [XS[ri]])
        pcarry = (xcr, xci, Sp)
        jobs = []
        for t in range(NPRE):
            f = lambda slot, xtb, t=t: L0[slot](x_pre[t * 128:(t + 1) * 128, :], None, None, 127, False, so=True, carry=pcarry, xtb=xtb)
            f.xsrc = (x_pre[t * 128:(t + 1) * 128, :], 128, []); jobs.append(f)

        def post_prompt():
            T.dma('sp', bs(xcr), o_s5r_p, xcr.t[:], reads=[xcr.b]); T.dma('sp', bs(xci), o_s5i_p, xci.t[:], reads=[xci.b])
            for hp in range(2):
                T.dma('sp', bs(Sp[hp]), o_gla_p[hp], Sp[hp].t[:], reads=[Sp[hp].b])
        mjobs = []
        for t in range(NMAIN + 1):
            f = lambda slot, xtb, t=t: L0[slot](x_main[t * 128:(t + 1) * 128, :], h1_p[t * 128:(t + 1) * 128, :], Bh1p, 127, False,
                                                carry=pcarry, post=(post_prompt if t == NMAIN else None), xtb=xtb)
            f.xsrc = (x_main[t * 128:(t + 1) * 128, :], 128, []); mjobs.append(f)
        sjobs = []
        if with_sample:
            for s in range(NSMP):
                def pre_s(s=s):
                    assert not scar_busy[s % 2], "sample carry buffers still in use"
                    scar_busy[s % 2] = True
                    cr, ci, Ss = scar[s % 2]
                    cp(cr.t[:], XS[0].t[:, :, s], [XS[0]], [cr]); cp(ci.t[:], XS[1].t[:, :, s], [XS[1]], [ci])
                    for hp in range(2):
                        T.dma('sp', bs(Ss[hp]), Ss[hp].t[:], st_gla[s, 2 * hp:2 * hp + 2].rearrange("h k v -> (h k) v"), writes=[Ss[hp].b])

                def post_s(s=s):
                    scar_busy[s % 2] = False
                    cr, ci, Ss = scar[s % 2]
                    cp(XO[0].t[:, :, s], cr.t[:], [cr], [XO[0]]); cp(XO[1].t[:, :, s], ci.t[:], [ci], [XO[1]])
                    for hp in range(2):
                        T.dma('sp', bs(Ss[hp]), o_gla_s[s, 2 * hp:2 * hp + 2].rearrange("h k v -> (h k) v"), Ss[hp].t[:], reads=[Ss[hp].b])
                f = lambda slot, xtb, s=s, pre_s=pre_s, post_s=post_s: L0[slot](
                    x_s[s * 128:(s + 1) * 128, :], h1_s[4 * s:4 * s + 4, :], Bh1s, 3, True,
                    carry=scar[s % 2], pre=pre_s, post=post_s, xtb=xtb)
                f.xsrc = (x_s[s * 128:(s + 1) * 128, :], 128, []); sjobs.append(f)
        if len(jobs) % 2:
            jobs.append(mjobs.pop(0))
        while mjobs or sjobs:
            if mjobs:
                jobs.append(mjobs.pop(0))
            if sjobs:
                jobs.append(sjobs.pop(0))
        run_jobs(jobs, XT_A)
        if with_sample:
            for ri, dst in ((0, o_s5r_s), (1, o_s5i_s)):
                for q4 in range(4):
                    for gg in range(4):
                        gpair = 4 * q4 + gg
                        p = sps()
                        tr(p, p.t[0:NSMP, 0:128], XO[ri].t[:, gpair, :], idf.t[:], [XO[ri], idf])
                        cp(stout.t[:, gg * 128:(gg + 1) * 128], p.t[0:NSMP, 0:128], [p], [stout])
                    T.dma('sp', bs(stout), dst[:, q4 * 512:(q4 + 1) * 512], stout.t[:], reads=[stout.b])

        T.barrier()
        esA.close(); cur_es[0] = es
        w_in1 = W.t[:, 0:8 * OIN].rearrange("p (k c) -> p k c", k=8)
        w_out1 = W.t[:, 8 * OIN:8 * OIN + 8 * D].rearrange("p (k c) -> p k c", k=8)
        for k in range(8):
            T.dma('pool', s_w, w_in1[:, k, :], o_win[k * 128:(k + 1) * 128, :], writes=[W.b, W2b], serial=False)
        for k in range(8):
            T.dma('pool', s_w, w_out1[:, k, :], o_wout[k * 128:(k + 1) * 128, :], writes=[W.b, W2b], serial=False)
        ld(g0, g0.t[:], o_ng.partition_broadcast(128)); g1 = g0
        gq_b = sb([128, 64]); ld(gq_b, gq_b.t[:], qg.partition_broadcast(128))
        gk_b = sb([128, 64]); ld(gk_b, gk_b.t[:], kg.partition_broadcast(128))
        esk = sb([128, 16]); ld(esk, esk.t[:], sinks.partition_broadcast(128))
        act(esk.t[:], esk.t[:], AF.Exp, [esk], [esk])
        mk4 = {}
        for nm, src in (("prev", m_prev), ("cur", m_cur), ("prev0", m_prev0)):
            mt = sb([128, 4, 128], BF16)
            for j in range(4):
                ld(mt, mt.t[:, j, :], src)
            mk4[nm] = mt
        kTs = [sb([128, 128], BF16) for _ in range(3)]
        vas = [sb([128, 2, 68], BF16) for _ in range(3)]
        for va in vas:
            V('pool', lambda e, va=va: e.memset(va.t[:], 1.0), [], [va])

        ckT_all = sb([128, NSMP, 128], BF16); cva_all = sb([128, NSMP, 2, 68], BF16)
        V('pool', lambda e: e.memset(cva_all.t[:], 1.0), [], [cva_all])
        Bm = sb([128, 124], BF16); ld(Bm, Bm.t[:], m_sc)
        msn = sb([128, 64], BF16); ld(msn, msn.t[:], m_sn)

        class _V:
            def __init__(self, tb, ap):
                self.t = ap; self.b = tb.b

        XT_B = [sb([128, D], name="xtB") for _ in range(3)]
        for xb_ in XT_B:
            V('pool', lambda e, xb_=xb_: e.memset(xb_.t[:], 0.0), [], [xb_])

        def make_l1(slot):
            xn = sb([128, D], BF16); xnT = sb([128, 8, 128], BF16)
            ssq = sb([128, 1]); rstd = sb([128, 1]); junk = xn
            rr = [0]

            def nps():
                p = PS[4 * slot + 1 + rr[0] % 2]
                rr[0] += 1
                return p
            PACC = (PS[4 * slot], PS[4 * slot + 3])
            qk = sb([128, 18, 64]); sq1 = sb([128, 18, 64]); ss18 = sb([128, 18]); qr = sb([128, 18, 64])
            V('dve', lambda e: e.memset(qk.t[:], 0.0), [], [qk])
            ra = sb([128, 18, 32]); rb = sb([128, 18, 32]); rp = sb([128, 64])
            qp = sb([128, 16, 128], BF16); V('pool', lambda e: e.memset(qp.t[:], 0.0), [], [qp])
            kb16 = sb([128, 128], BF16); qpT = sb([128, 16, 128], BF16)
            kfp = sb([128, 128]); vfp = sb([128, 128]); sg1 = sb([128, 1024], BF16)
            pex = [sb([128, 512], BF16) for _ in range(2)]
            dn = sb([128, 4]); o4 = sb([128, 4, 64]); og = sb([128, 1024], BF16); ogT = sb([128, 8, 128], BF16)
            tg1 = sb([128, 512]); tg2 = sb([128, 512])
            yt = sb([128, D])
            ckls = [sb([128, 128]) for _ in range(4)]; cvls = [sb([128, 128]) for _ in range(4)]; ckbs = [sb([128, 128], BF16) for _ in range(4)]
            ckl = ckls[0]; cvl = cvls[0]; ckb = ckbs[0]
            skT = sb([128, 128], BF16); sva = sb([128, 2, 68], BF16); ckT = sb([128, 128], BF16); cva = sb([128, 2, 68], BF16)
            V('pool', lambda e: e.memset(sva.t[:], 1.0), [], [sva]); V('pool', lambda e: e.memset(cva.t[:], 1.0), [], [cva])

            def rmsnorm_T(xt, gt):
                act(junk.t[:], xt.t[:], AF.Square, [xt], [junk, ssq], accum_out=ssq.t[:])
                act(rstd.t[:], ssq.t[:], AF.Ln, [ssq], [rstd], scale=1.0 / D, bias=EPS)
                act(rstd.t[:], rstd.t[:], AF.Exp, [rstd], [rstd], scale=-0.5)
                stt(xn.t[:], xt.t[:], rstd.t[:, 0:1], gt.t[:], MUL, MUL, [xt, rstd, gt], [xn])
                p = nps()
                pb = p.t[:].bitcast(BF16)
                for k in range(8):
                    tr(p, pb[:, k * 128:(k + 1) * 128], xn.t[:, k * 128:(k + 1) * 128], idb.t[:], [xn, idb], signal=(k == 7))
                cp(xnT.t[:].rearrange("p k t -> p (k t)"), pb[:, 0:1024], [p], [xnT], E='act')

            def layer1_tile(hsrc, hbuf, rope_src, prev, cur, ydst, yrows, kvonly=False, pmask='prev', smp=None, xtb=None):
                xt = xtb
                if smp is not None and smp != 'packed':
                    s = smp
                    prev = (ckT, cva); cur = (skT, sva)
                    T.dma('sp', bs(ckl), ckl.t[:], ck_in[s], writes=[ckl.b]); T.dma('sp', bs(cvl), cvl.t[:], cv_in[s], writes=[cvl.b])
                    cp(ckb.t[:], ckl.t[:], [ckl], [ckb], E='act')
                    p = nps(); pb = p.t[:].bitcast(BF16)
                    tr(p, pb[:, 0:128], ckb.t[:], idb.t[:], [ckb, idb])
                    cp(ckT.t[:], pb[:, 0:128], [p], [ckT])
                    cp(cva.t[:, :, 0:64], cvl.t[:].rearrange("p (h d) -> p h d", h=2), [cvl], [cva])
                packed = (smp == 'packed')
                if packed:
                    cur = (skT, sva)
                    for s in range(NSMP):
                        ckl = ckls[s % 4]; cvl = cvls[s % 4]; ckb = ckbs[s % 4]
                        T.dma('sp', bs(ckl), ckl.t[:], ck_in[s], writes=[ckl.b]); T.dma('sp', bs(cvl), cvl.t[:], cv_in[s], writes=[cvl.b])
                        cp(ckb.t[:], ckl.t[:], [ckl], [ckb], E='act')
                        p = nps(); pb = p.t[:].bitcast(BF16)
                        tr(p, pb[:, 0:128], ckb.t[:], idb.t[:], [ckb, idb])
                        cp(ckT_all.t[:, s, :], pb[:, 0:128], [p], [ckT_all])
                        cp(cva_all.t[:, s, :, 0:64], cvl.t[:].rearrange("p (h d) -> p h d", h=2), [cvl], [cva_all])
                        T.dma('sp', s_out, o_k_s[s, 0:124, :], ck_in[s, 4:128, :], serial=False)
                        T.dma('sp', s_out, o_v_s[s, 0:124, :], cv_in[s, 4:128, :], serial=False)
                        if s % 4 == 3:
                            yield 4.0
                kT_c, va_c = cur
                rmsnorm_T(xt, g1)
                T.dma('sp', bs(rp), rp.t[:], rope_src, writes=[rp.b])
                yield 3.0
                for half in (() if kvonly else range(2)):
                    pq = nps()
                    for k in range(8):
                        mm(pq, pq.t[:, :], xnT.t[:, k, :], w_in1[:, k, half * 512:(half + 1) * 512], [xnT, W],
                           start=(k == 0), stop=(k == 7), signal=(k == 7))
                    cp(qk.t[:, 8 * half:8 * half + 8, :].rearrange("p h d -> p (h d)"), pq.t[:], [pq], [qk], E=('dve' if half else 'act'))
                    yield 2.3
                pkv = nps()
                for k in range(8):
                    mm(pkv, pkv.t[:, :], xnT.t[:, k, :], w_in1[:, k, 1024:1536], [xnT, W], start=(k == 0), stop=(k == 7), signal=(k == 7))
                cp(qk.t[:, 16:18, :].rearrange("p h d -> p (h d)"), pkv.t[:, 0:128], [pkv], [qk])
                cp(vfp.t[:], pkv.t[:, 128:256], [pkv], [vfp], E='act')
                for h_ in range(2):
                    cp(va_c.t[:, h_, 0:64], vfp.t[:, h_ * 64:(h_ + 1) * 64], [vfp], [va_c])
                yield 2.3
                for half in (() if kvonly else range(2)):
                    pg = nps()
                    for k in range(8):
                        mm(pg, pg.t[:, :], xnT.t[:, k, :], w_in1[:, k, 1280 + half * 512:1280 + (half + 1) * 512], [xnT, W],
                           start=(k == 0), stop=(k == 7), signal=(k == 7))
                    act(tg1.t[:], pg.t[:], AF.Tanh, [pg], [tg1], scale=0.5); act(tg2.t[:], pg.t[:], AF.Identity, [pg], [tg2], scale=0.5)
                    stt(sg1.t[:, half * 512:(half + 1) * 512], tg1.t[:], 1.0, tg2.t[:], ADD, MUL, [tg1, tg2], [sg1])
                    yield 3.3
                yield 'A'
                act(sq1.t[:], qk.t[:], AF.Square, [qk], [sq1])
                yield 0
                V('dve', lambda e: e.tensor_reduce(out=ss18.t[:], in_=sq1.t[:], axis=AX.X, op=ADD), [sq1], [ss18])
                yield 0
                act(ss18.t[:], ss18.t[:], AF.Ln, [ss18], [ss18], scale=1.0 / 64, bias=EPS)
                act(ss18.t[:], ss18.t[:], AF.Exp, [ss18], [ss18], scale=-0.5)
                yield 0
                tt(qk.t[:], qk.t[:], ss18.t[:].unsqueeze(2).to_broadcast([128, 18, 64]), MUL, [qk, ss18], [qk])
                tt(qk.t[:, 0:16, :], qk.t[:, 0:16, :], gq_b.t[:].unsqueeze(1).to_broadcast([128, 16, 64]), MUL, [qk, gq_b], [qk])
                tt(qk.t[:, 16:18, :], qk.t[:, 16:18, :], gk_b.t[:].unsqueeze(1).to_broadcast([128, 2, 64]), MUL, [qk, gk_b], [qk])
                cosb = rp.t[:, 0:32].unsqueeze(1).to_broadcast([128, 18, 32]); sinb = rp.t[:, 32:64].unsqueeze(1).to_broadcast([128, 18, 32])
                x1 = qk.t[:, :, 0:32]; x2 = qk.t[:, :, 32:64]
                tt(ra.t[:], x1, cosb, MUL, [qk, rp], [ra]); tt(rb.t[:], x2, sinb, MUL, [qk, rp], [rb])
                tt(qr.t[:, :, 0:32], ra.t[:], rb.t[:], SUB, [ra, rb], [qr])
                tt(ra.t[:], x2, cosb, MUL, [qk, rp], [ra]); tt(rb.t[:], x1, sinb, MUL, [qk, rp], [rb])
                tt(qr.t[:, :, 32:64], ra.t[:], rb.t[:], ADD, [ra, rb], [qr])
                yield 0
                cp(kfp.t[:], qr.t[:, 16:18, :].rearrange("p h d -> p (h d)"), [qr], [kfp], E='act')
                cp(kb16.t[:], kfp.t[:], [kfp], [kb16], E='act')
                p = nps(); pb = p.t[:].bitcast(BF16)
                tr(p, pb[:, 0:128], kb16.t[:], idb.t[:], [kb16, idb])
                cp(kT_c.t[:], pb[:, 0:128], [p], [kT_c])
                if kvonly:
                    return
                cp(qp.t[:, 0:8, 0:64], qr.t[:, 0:8, :], [qr], [qp]); cp(qp.t[:, 8:16, 64:128], qr.t[:, 8:16, :], [qr], [qp])
                yield 0
                for half in range(2):
                    p = nps(); pb = p.t[:].bitcast(BF16)
                    for j in range(8):
                        tr(p, pb[:, j * 128:(j + 1) * 128], qp.t[:, 8 * half + j, :], idb.t[:], [qp, idb], signal=(j == 7))
                    cp(qpT.t[:, 8 * half:8 * half + 8, :].rearrange("p h t -> p (h t)"), pb[:, 0:1024], [p], [qpT], E=('act' if half else 'dve'))
                yield 0
                blocks = ([(pmask, prev[0], prev[1])] if prev is not None else []) + [("cur", kT_c, va_c)]
                if packed:
                    NQ = 64
                    for hg in range(4):
                        kvh = hg // 2
                        pacc = PACC[hg % 2]
                        qv4 = qpT.t[:, 4 * hg:4 * hg + 4, 0:NQ]
                        for bi in range(NSMP + 1):
                            if bi < NSMP:
                                kT_b = _V(ckT_all, ckT_all.t[:, bi, :]); va_ap = cva_all.t[:, bi, kvh, :]; va_tb = cva_all
                                mk = Bm.t[:, 60 - 4 * bi:60 - 4 * bi + NQ].unsqueeze(1).to_broadcast([128, 4, NQ]); mk_tb = Bm
                            else:
                                kT_b = kT_c; va_ap = va_c.t[:, kvh, :]; va_tb = va_c
                                mk = msn.t[:].unsqueeze(1).to_broadcast([128, 4, NQ]); mk_tb = msn
                            p = nps()
                            pv3 = p.t[:, 0:4 * NQ].rearrange("p (h q) -> p h q", h=4)
                            mm(p, pv3, kT_b.t[:], qv4, [kT_b, qpT], start=True, stop=False, signal=False)
                            mm(p, pv3, idb.t[:], mk, [idb, mk_tb], start=False, stop=True)
                            px = pex[bi % 2]
                            act(px.t[:, 0:4 * NQ], p.t[:, 0:4 * NQ], AF.Exp, [p], [px], scale=0.125)
                            for j in range(4):
                                mm(pacc, pacc.t[0:NQ, j * 68:(j + 1) * 68], px.t[:, j * NQ:(j + 1) * NQ], va_ap, [px, va_tb],
                                   start=(bi == 0 and j == 0), stop=(bi == NSMP), signal=(j == 3))
                            yield 1.2
                        pav = pacc.t[:, 0:272].rearrange("p (h e) -> p h e", h=4)
                        tt(dn.t[:], pav[:, :, 64], esk.t[:, 4 * hg:4 * hg + 4], ADD, [pacc, esk], [dn])
                        V('dve', lambda e: e.reciprocal(out=dn.t[:], in_=dn.t[:]), [dn], [dn])
                        tt(o4.t[:], pav[:, :, 0:64], dn.t[:].unsqueeze(2).to_broadcast([128, 4, 64]), MUL, [pacc, dn], [o4])
                        tt(og.t[:, hg * 256:(hg + 1) * 256], o4.t[:].rearrange("p h d -> p (h d)"), sg1.t[:, hg * 256:(hg + 1) * 256], MUL, [o4, sg1], [og])
                        yield 1.0
                for hg in (() if packed else range(4)):
                    kvh = hg // 2
                    for bi, (nm, kT_b, va_b) in enumerate(blocks):
                        p = nps()
                        mm(p, p.t[:, :], kT_b.t[:], qpT.t[:, 4 * hg:4 * hg + 4, :].rearrange("p h t -> p (h t)"), [kT_b, qpT],
                           start=True, stop=False, signal=False)
                        mm(p, p.t[:, :], idb.t[:], mk4[nm].t[:].rearrange("p h t -> p (h t)"), [idb, mk4[nm]], start=False, stop=True)
                        act(pex[bi].t[:], p.t[:], AF.Exp, [p], [pex[bi]], scale=0.125)
                    yield 2.0
                    pacc = PACC[hg % 2]
                    for j in range(4):
                        for bi, (nm, kT_b, va_b) in enumerate(blocks):
                            mm(pacc, pacc.t[:, j * 68:(j + 1) * 68], pex[bi].t[:, j * 128:(j + 1) * 128], va_b.t[:, kvh, :], [pex[bi], va_b],
                               start=(bi == 0), stop=(bi == len(blocks) - 1), signal=(bi == len(blocks) - 1 and j == 3))
                    yield 0
                    pav = pacc.t[:, 0:272].rearrange("p (h e) -> p h e", h=4)
                    tt(dn.t[:], pav[:, :, 64], esk.t[:, 4 * hg:4 * hg + 4], ADD, [pacc, esk], [dn])
                    V('dve', lambda e: e.reciprocal(out=dn.t[:], in_=dn.t[:]), [dn], [dn])
                    tt(o4.t[:], pav[:, :, 0:64], dn.t[:].unsqueeze(2).to_broadcast([128, 4, 64]), MUL, [pacc, dn], [o4])
                    tt(og.t[:, hg * 256:(hg + 1) * 256], o4.t[:].rearrange("p h d -> p (h d)"), sg1.t[:, hg * 256:(hg + 1) * 256], MUL, [o4, sg1], [og])
                    yield 0
                p = nps(); pb = p.t[:].bitcast(BF16)
                for k in range(8):
                    tr(p, pb[:, k * 128:(k + 1) * 128], og.t[:, k * 128:(k + 1) * 128], idb.t[:], [og, idb], signal=(k == 7))
                yield 0
                cp(ogT.t[:].rearrange("p k t -> p (k t)"), pb[:, 0:1024], [p], [ogT], E='act')
                yield 0
                for half in range(2):
                    p = nps()
                    for k in range(8):
                        mm(p, p.t[:, :], ogT.t[:, k, :], w_out1[:, k, half * 512:(half + 1) * 512], [ogT, W],
                           start=(k == 0), stop=(k == 7), signal=(k == 7))
                    tt(yt.t[:, half * 512:(half + 1) * 512], p.t[:], xt.t[:, half * 512:(half + 1) * 512], ADD, [p, xt], [yt])
                T.dma('sp', bs(yt), ydst, yt.t[0:yrows, :], reads=[yt.b])
                if packed:
                    for s in range(NSMP):
                        T.dma('sp', s_out, o_k_s[s, 124:128, :], kfp.t[4 * s:4 * s + 4, :], reads=[kfp.b], serial=False)
                        T.dma('sp', s_out, o_v_s[s, 124:128, :], vfp.t[4 * s:4 * s + 4, :], reads=[vfp.b], serial=False)
                elif ydst is y_last:
                    T.dma('sp', bs(kfp), o_k_p, kfp.t[:], reads=[kfp.b]); T.dma('sp', bs(vfp), o_v_p, vfp.t[:], reads=[vfp.b])
            return layer1_tile

        L1 = [make_l1(0), make_l1(1)]
        y_last = y_p[(NMAIN - 1) * 128:NMAIN * 128, :]
        jobs = []
        for t in range(NMAIN + 1):
            cur = (kTs[t % 3], vas[t % 3]); prevb = (kTs[(t - 1) % 3], vas[(t - 1) % 3])
            if t == 0:
                f = lambda slot, xtb, cur=cur: L1[slot](h1_p[0:128, :], Bh1p, rope_p[0:128, :], None, cur, None, 0, kvonly=True, xtb=xtb)
                f.xsrc = (h1_p[0:128, :], 128, [Bh1p]); jobs.append(f)
            else:
                ydst = y_last if t == NMAIN else y_p[(t - 1) * 128:t * 128, :]
                f = lambda slot, xtb, t=t, cur=cur, prevb=prevb, ydst=ydst: L1[slot](
                    h1_p[t * 128:(t + 1) * 128, :], Bh1p, rope_p[t * 128:(t + 1) * 128, :], prevb, cur, ydst, 128,
                    pmask=('prev0' if t == 1 else 'prev'), xtb=xtb)
                f.xsrc = (h1_p[t * 128:(t + 1) * 128, :], 128, [Bh1p]); jobs.append(f)
        if with_sample:
            f = lambda slot, xtb: L1[slot](h1_s, Bh1s, rope_s, None, None, y_s, 64, smp='packed', xtb=xtb)
            f.xsrc = (h1_s, 64, [Bh1s]); jobs.insert(0, f)
        run_jobs(jobs, XT_B)
        T.finish('sp')
    return nc


def _prep_common(inputs):
    f = lambda k: np.ascontiguousarray(np.asarray(inputs[k], dtype=np.float32))
    c = {}
    c["e_ng"] = f("even_norm_g").reshape(1, D); c["e_win"] = f("even_w_in")[0]
    c["lam_re"] = f("s5_lambda_re")[0]; c["lam_im"] = f("s5_lambda_im")[0]; c["log_dt"] = f("s5_log_dt").reshape(1, 32)
    c["b_re"] = f("s5_b_re")[0]; c["b_im"] = f("s5_b_im")[0]
    c["c_re"] = f("s5_c_re")[0].reshape(512, 64); c["c_im"] = f("s5_c_im")[0].reshape(512, 64)
    c["s5_d"] = f("s5_d").reshape(512, 1); c["w_glu"] = f("s5_w_glu")[0]; c["b_glu"] = f("s5_b_glu").reshape(512, 1)
    c["w_gate"] = f("gla_w_gate")[0]; c["b_gate"] = f("gla_b_gate").reshape(256, 1); c["gla_g"] = f("gla_norm_g").reshape(128, 1)
    c["e_wout"] = f("even_w_out")[0]
    c["o_ng"] = f("odd_norm_g").reshape(1, D); c["o_win"] = f("odd_w_in")[0]
    c["qg"] = f("swa_q_norm_g").reshape(1, 64); c["kg"] = f("swa_k_norm_g").reshape(1, 64)
    c["sinks"] = f("swa_sinks").reshape(1, 16); c["o_wout"] = f("odd_w_out")[0]
    return c


def _const_tables():
    c = {}
    half = 32
    inv = (10000.0 ** (-np.arange(half, dtype=np.float32) / half)).astype(np.float32)
    def rope(pos):
        ang = pos.astype(np.float32)[:, None] * inv[None, :]
        return np.concatenate([np.cos(ang), np.sin(ang)], axis=1).astype(np.float32)
    c["_rope"] = rope
    c["rope_s"] = rope(8192 + (np.arange(128) % 4))
    vm = np.zeros((1, 128), np.float32); vm[0, :4] = 1.0
    c["vmask_s"] = vm
    i = np.arange(128)
    same = (i[:, None] // 64) == (i[None, :] // 64)
    c["m_gla"] = (same & (i[:, None] <= i[None, :])).astype(np.float32)
    j = np.arange(64)
    c["m_gla_s"] = (((j[:, None] // 4) == (j[None, :] // 4)) & (j[:, None] <= j[None, :])).astype(np.float32)
    BIG = -240000.0
    bf = ml_dtypes.bfloat16
    c["m_prev"] = np.where(i[:, None] > i[None, :], 0.0, BIG).astype(bf)
    c["m_cur"] = np.where(i[:, None] <= i[None, :], 0.0, BIG).astype(bf)
    c["_m_none"] = np.full((128, 128), BIG, np.float32).astype(bf)
    cc = np.arange(128)[:, None]; xx = (np.arange(124) - 60)[None, :]
    c["m_sc"] = np.where((xx >= 0) & (xx < 4) & (cc > xx), 0.0, BIG).astype(bf)
    qq = np.arange(64)[None, :]
    c["m_sn"] = np.where((cc < 64) & ((cc // 4) == (qq // 4)) & (cc <= qq), 0.0, BIG).astype(bf)
    c["identf"] = np.eye(128, dtype=np.float32)
    r = np.ones((1, 128), np.float32); r[0, 0] = 0; r[0, 64] = 0
    c["rmask"] = r
    c["rmask_s"] = np.ones((1, 64), np.float32)
    c["krev"] = (127.0 - np.arange(128, dtype=np.float32)).reshape(1, 128)
    c["seqsel"] = np.zeros((64, NSMP), np.float32)
    return c


_CACHE = {}


def core_inputs(inputs, com, c, NPRE, NMAIN, seq_tiles, b=None, j=None):
    if b is None:
        b, j = c // 4, c % 4
    xp = np.asarray(inputs["x_prompt"], np.float32); xs = np.asarray(inputs["x_sample"], np.float32)
    m = {k: v for k, v in com.items() if not k.startswith("_")}
    first = j * NMAIN - 1
    npre_real = max(first, 0)
    xpre = np.zeros((max(NPRE, 1) * 128, D), np.float32)
    if npre_real:
        xpre[(NPRE - npre_real) * 128:NPRE * 128] = xp[b, 0:npre_real * 128]
    xmain = np.zeros(((NMAIN + 1) * 128, D), np.float32)
    lo = first * 128
    if first >= 0:
        xmain[:] = xp[b, lo:lo + (NMAIN + 1) * 128]
    else:
        xmain[128:] = xp[b, 0:NMAIN * 128]
    m["x_pre"] = xpre; m["x_main"] = xmain
    m["rope_p"] = com["_rope"](np.maximum(lo + np.arange((NMAIN + 1) * 128), 0))
    m["m_prev0"] = com["m_prev"] if first >= 0 else com["_m_none"]
    sl = slice(c * NSMP, (c + 1) * NSMP)
    xpad = np.zeros((NSMP, 128, D), np.float32); xpad[:, :4] = xs[sl]
    m["x_s"] = xpad.reshape(NSMP * 128, D)
    m["st_s5r"] = np.asarray(inputs["state_s5_re"], np.float32)[0, sl].reshape(NSMP, 2048)
    m["st_s5i"] = np.asarray(inputs["state_s5_im"], np.float32)[0, sl].reshape(NSMP, 2048)
    m["st_gla"] = np.asarray(inputs["state_gla"], np.float32)[0, sl]
    m["ck_in"] = np.asarray(inputs["cache_swa_k"], np.float32)[0, sl].reshape(NSMP, 128, 128)
    m["cv_in"] = np.asarray(inputs["cache_swa_v"], np.float32)[0, sl].reshape(NSMP, 128, 128)
    return {k: np.ascontiguousarray(v) for k, v in m.items()}


def kernel(**inputs):
    NMAIN = SEQ // 128 // 4
    NPRE = 3 * NMAIN - 1
    if "nc" not in _CACHE:
        _CACHE["nc"] = build_program(NPRE, NMAIN)
    nc = _CACHE["nc"]
    com = _prep_common(inputs)
    com.update(_const_tables())
    in_maps = [core_inputs(inputs, com, c, NPRE, NMAIN, SEQ // 128) for c in range(NCORES)]
    res = run_bass_kernel_spmd(nc, in_maps, core_ids=list(range(NCORES)))
    R = res.results
    y_p = np.stack([np.concatenate([R[4 * b + j]["y_p"] for j in range(4)]) for b in range(2)])
    y_s = np.concatenate([R[c]["y_s"].reshape(NSMP, 4, D) for c in range(NCORES)])
    last = [3, 7]
    def s5(name):
        return np.stack([R[c][name].reshape(2, 64, 16).transpose(2, 0, 1).reshape(32, 64) for c in last])[None]
    gla_p = np.stack([R[c]["o_gla_p"].reshape(4, 64, 128) for c in last])[None]
    k_p = np.stack([R[c]["o_k_p"].reshape(128, 2, 64) for c in last])[None]
    v_p = np.stack([R[c]["o_v_p"].reshape(128, 2, 64) for c in last])[None]
    cat = lambda name, shp: np.concatenate([R[c][name].reshape((NSMP,) + shp) for c in range(NCORES)])[None]
    return (y_p, y_s, s5("o_s5r_p"), s5("o_s5i_p"), gla_p, k_p, v_p,
            cat("o_s5r_s", (32, 64)), cat("o_s5i_s", (32, 64)), cat("o_gla_s", (4, 64, 128)),
            cat("o_k_s", (128, 2, 64)), cat("o_v_s", (128, 2, 64)))
```
